# Optimizing a Trainium2 kernel written in Bass

```python
import math
import jax, jax.numpy as jnp
from jax import lax
import numpy as np

D_MODEL = 2048
BATCH = 1
SEQ = 8192
DEPTH = 1
DEC_BATCH = 32
DEC_SEQ = 32
PAST_LEN = 4096

CHUNK = 64
MIX_W = D_MODEL
RET_HEADS = 4
RET_DK = 256
RET_DV = 256
RET_W = RET_HEADS * RET_DV
MLA_HEADS = 8
Q_LORA = 512
KV_LORA = 512
NOPE_DIM = 128
ROPE_DIM = 64
V_HEAD = 128
QK_HEAD = NOPE_DIM + ROPE_DIM
MLA_W = MLA_HEADS * V_HEAD
ROPE_BASE = 10000.0
Q_BLOCK = 128
NEG_INF = -1e30
N_KEYS = 128
N_EXPERTS = N_KEYS * N_KEYS
PEER_HEADS = 8
PEER_QDIM = 256
PEER_HALF = PEER_QDIM // 2
PEER_TOPK = 16
PEER_BLOCK = 128
PLE_DIM = 256
EPS = 1e-6
OFF_RQ = 0
OFF_RK = OFF_RQ + RET_HEADS * RET_DK
OFF_RV = OFF_RK + RET_HEADS * RET_DK
OFF_RG = OFF_RV + RET_W
OFF_CQ = OFF_RG + RET_W
OFF_CKV = OFF_CQ + Q_LORA
OFF_KR = OFF_CKV + KV_LORA
IN_W = OFF_KR + ROPE_DIM

kernel_name = 'hymba_retnet_mla_peer_stream_step'


def rmsnorm(x, g):
    xf = x.astype(jnp.float32)
    y = xf * lax.rsqrt(jnp.mean(xf * xf, axis=-1, keepdims=True) + EPS)
    return (y * g.astype(jnp.float32)).astype(x.dtype)


def rope(x, pos):
    d = x.shape[-1]
    inv = ROPE_BASE ** (-jnp.arange(0, d, 2, dtype=jnp.float32) / d)
    ang = pos.astype(jnp.float32)[:, None] * inv[None, :]
    cos = jnp.cos(ang)[:, None, :]
    sin = jnp.sin(ang)[:, None, :]
    xf = x.astype(jnp.float32)
    x1, x2 = xf[..., : d // 2], xf[..., d // 2:]
    return jnp.concatenate([x1 * cos - x2 * sin, x2 * cos + x1 * sin], axis=-1).astype(x.dtype)


def retention_log_decay():
    return jnp.log1p(-jnp.exp(jnp.linspace(math.log(1.0 / 32), math.log(1.0 / 512), RET_HEADS, dtype=jnp.float32)))


def retention_chunk(q, k, v, s_prev, log_g):
    L = q.shape[1]
    idx = jnp.arange(L)
    diff = idx[:, None] - idx[None, :]
    dmask = jnp.where(diff[None] >= 0,
                      jnp.exp(jnp.maximum(diff, 0).astype(jnp.float32)[None] * log_g[:, None, None]), 0.0)
    scores = jnp.einsum('blhd,bmhd->bhlm', q, k) * dmask[None]
    intra = jnp.einsum('bhlm,bmhe->blhe', scores, v)
    inner = jnp.exp((idx + 1).astype(jnp.float32)[:, None] * log_g[None, :])
    cross = jnp.einsum('blhd,bhde->blhe', q, s_prev) * inner[None, :, :, None]
    kdec = jnp.exp((L - 1 - idx).astype(jnp.float32)[:, None] * log_g[None, :])
    s_new = jnp.exp(L * log_g)[None, :, None, None] * s_prev + \
        jnp.einsum('blhd,blhe->bhde', k * kdec[None, :, :, None], v)
    return intra + cross, s_new


def retention(q, k, v, s0):
    B, T, H, _ = q.shape
    L = min(T, CHUNK)
    nc = T // L
    log_g = retention_log_decay()

    def to_chunks(t):
        return t.reshape(B, nc, L, H, t.shape[-1]).transpose(1, 0, 2, 3, 4)

    def step(s, qkv):
        qc, kc, vc = qkv
        o, s = retention_chunk(qc, kc, vc, s, log_g)
        return s, o

    s_fin, o = lax.scan(step, s0, (to_chunks(q), to_chunks(k), to_chunks(v)))
    o = o.transpose(1, 0, 2, 3, 4).reshape(B, T, H, v.shape[-1])
    return o, s_fin


def head_groupnorm(o, g):
    mu = jnp.mean(o, axis=-1, keepdims=True)
    var = jnp.mean(jnp.square(o - mu), axis=-1, keepdims=True)
    y = (o - mu) * lax.rsqrt(var + EPS)
    B, T = o.shape[:2]
    return y.reshape(B, T, -1) * g.astype(jnp.float32)


def mla_prompt_attention(q_nope, q_rope, c_kv, k_rope, w_uk, w_uv):
    B, S, H, _ = q_nope.shape
    k_nope = jnp.einsum('bsc,chd->bshd', c_kv, w_uk)
    v = jnp.einsum('bsc,chd->bshd', c_kv, w_uv)
    nb = S // Q_BLOCK
    k_chunk = jnp.arange(S) // CHUNK
    scale = QK_HEAD ** -0.5

    def blk(args):
        qn, qr, start = args
        s = (jnp.einsum('bqhd,bkhd->bhqk', qn, k_nope) +
             jnp.einsum('bqhr,bkr->bhqk', qr, k_rope)).astype(jnp.float32) * scale
        q_chunk = (start + jnp.arange(Q_BLOCK)) // CHUNK
        mask = k_chunk[None, :] <= q_chunk[:, None]
        p = jax.nn.softmax(jnp.where(mask[None, None], s, NEG_INF), axis=-1)
        return jnp.einsum('bhqk,bkhd->bqhd', p.astype(v.dtype), v)

    qn_b = q_nope.reshape(B, nb, Q_BLOCK, H, NOPE_DIM).transpose(1, 0, 2, 3, 4)
    qr_b = q_rope.reshape(B, nb, Q_BLOCK, H, ROPE_DIM).transpose(1, 0, 2, 3, 4)
    starts = jnp.arange(nb, dtype=jnp.int32) * Q_BLOCK
    o = lax.map(blk, (qn_b, qr_b, starts))
    return o.transpose(1, 0, 2, 3, 4).reshape(B, S, H, V_HEAD)


def mla_sample_attention(q_nope, q_rope, c_new, kr_new, c_cache, kr_cache, w_uk, w_uv):
    scale = QK_HEAD ** -0.5
    P = c_cache.shape[1]
    q_lat = jnp.einsum('bthd,chd->bthc', q_nope, w_uk)
    s_past = jnp.einsum('bthc,bpc->bhtp', q_lat, c_cache) + jnp.einsum('bthr,bpr->bhtp', q_rope, kr_cache)
    s_new = jnp.einsum('bthc,bnc->bhtn', q_lat, c_new) + jnp.einsum('bthr,bnr->bhtn', q_rope, kr_new)
    s = jnp.concatenate([s_past, s_new], axis=-1).astype(jnp.float32) * scale
    p = jax.nn.softmax(s, axis=-1)
    ctx = jnp.einsum('bhtp,bpc->bthc', p[..., :P], c_cache.astype(jnp.float32)) + \
        jnp.einsum('bhtn,bnc->bthc', p[..., P:], c_new.astype(jnp.float32))
    return jnp.einsum('bthc,chd->bthd', ctx.astype(q_nope.dtype), w_uv)


def peer(m, w_pq, sub_keys, u_tab, v_tab):
    N, D = m.shape
    q = (m @ w_pq).reshape(N, PEER_HEADS, 2, PEER_HALF).astype(jnp.float32)
    sk = sub_keys.astype(jnp.float32)
    s1 = jnp.einsum('nhd,hkd->nhk', q[:, :, 0], sk[:, 0])
    s2 = jnp.einsum('nhd,hkd->nhk', q[:, :, 1], sk[:, 1])
    v1, i1 = lax.top_k(s1, PEER_TOPK)
    v2, i2 = lax.top_k(s2, PEER_TOPK)
    cand = (v1[..., :, None] + v2[..., None, :]).reshape(N, PEER_HEADS, PEER_TOPK * PEER_TOPK)
    cand_idx = (i1[..., :, None] * N_KEYS + i2[..., None, :]).reshape(N, PEER_HEADS, PEER_TOPK * PEER_TOPK)
    top_s, sel = lax.top_k(cand, PEER_TOPK)
    idx = jnp.take_along_axis(cand_idx, sel, axis=-1)
    g = jax.nn.softmax(top_s, axis=-1)
    pad = (-N) % PEER_BLOCK
    nb = (N + pad) // PEER_BLOCK
    m_p = jnp.pad(m, ((0, pad), (0, 0))).reshape(nb, PEER_BLOCK, D)
    idx_p = jnp.pad(idx, ((0, pad), (0, 0), (0, 0))).reshape(nb, PEER_BLOCK, PEER_HEADS, PEER_TOPK)
    g_p = jnp.pad(g, ((0, pad), (0, 0), (0, 0))).reshape(nb, PEER_BLOCK, PEER_HEADS, PEER_TOPK)

    def blk(args):
        mb, ib, gb = args
        u = u_tab[ib]
        h = jnp.einsum('nd,nhkd->nhk', mb, u).astype(jnp.float32)
        a = (gb * jax.nn.gelu(h, approximate=False)).astype(mb.dtype)
        return jnp.einsum('nhk,nhkd->nd', a, v_tab[ib])

    out = lax.map(blk, (m_p, idx_p, g_p))
    return out.reshape(nb * PEER_BLOCK, D)[:N]


def layer(h, p_l, pos, s0, mla_cache, w):
    (g_mix, w_in, g_q, w_uq, g_kv, w_uk, w_uv, g_ret_out, g_mla_out, w_out,
     g_ffn, w_pq, sub_keys, u_tab, v_tab, g_ple, w_ple_gate, w_ple_proj) = w
    B, T, D = h.shape
    a = rmsnorm(h, g_mix)
    z = a @ w_in
    rq = rope(z[..., OFF_RQ:OFF_RK].reshape(B, T, RET_HEADS, RET_DK), pos).astype(jnp.float32)
    rk = rope(z[..., OFF_RK:OFF_RV].reshape(B, T, RET_HEADS, RET_DK), pos).astype(jnp.float32) * (RET_DK ** -0.5)
    rv = z[..., OFF_RV:OFF_RG].reshape(B, T, RET_HEADS, RET_DV).astype(jnp.float32)
    rg = z[..., OFF_RG:OFF_CQ].astype(jnp.float32)
    o_ret, s_new = retention(rq, rk, rv, s0.astype(jnp.float32))
    o_ret = (jax.nn.silu(rg) * head_groupnorm(o_ret, g_ret_out)).astype(h.dtype)
    c_q = rmsnorm(z[..., OFF_CQ:OFF_CKV], g_q)
    qh = jnp.einsum('btc,chd->bthd', c_q, w_uq)
    q_nope = qh[..., :NOPE_DIM]
    q_rope = rope(qh[..., NOPE_DIM:], pos)
    c_kv = rmsnorm(z[..., OFF_CKV:OFF_KR], g_kv)
    k_rope = rope(z[..., OFF_KR:IN_W][:, :, None, :], pos)[:, :, 0, :]
    if mla_cache is None:
        o_mla = mla_prompt_attention(q_nope, q_rope, c_kv, k_rope, w_uk, w_uv)
    else:
        o_mla = mla_sample_attention(q_nope, q_rope, c_kv, k_rope, mla_cache[0], mla_cache[1], w_uk, w_uv)
    o_mla = rmsnorm(o_mla.reshape(B, T, MLA_W).astype(h.dtype), g_mla_out)
    h = h + jnp.concatenate([o_ret, o_mla], axis=-1) @ w_out
    h = h + peer(rmsnorm(h, g_ffn).reshape(B * T, D), w_pq, sub_keys, u_tab, v_tab).reshape(B, T, D)
    gate = jax.nn.sigmoid((rmsnorm(h, g_ple) @ w_ple_gate).astype(jnp.float32))
    h = h + (gate * (p_l @ w_ple_proj).astype(jnp.float32)).astype(h.dtype)
    return h, c_kv, k_rope, s_new.astype(h.dtype)


def setup_inputs(seed: int = 0) -> dict:
    key = jax.random.key(seed)
    ks = jax.random.split(key, 32)
    f32 = jnp.float32

    def nrm(k, shape, scale):
        return jax.random.normal(k, shape, f32) * scale

    def gain(k, shape):
        return 1.0 + 0.01 * jax.random.normal(k, shape, f32)

    return {
        'x_prompt': nrm(ks[0], (BATCH, SEQ, D_MODEL), 1.0),
        'x_sample': nrm(ks[1], (DEC_BATCH, DEC_SEQ, D_MODEL), 1.0),
        'cache_ckv': nrm(ks[2], (DEPTH, DEC_BATCH, PAST_LEN, KV_LORA), 1.0),
        'cache_krope': nrm(ks[3], (DEPTH, DEC_BATCH, PAST_LEN, ROPE_DIM), 1.0),
        'state_ret': nrm(ks[4], (DEPTH, DEC_BATCH, RET_HEADS, RET_DK, RET_DV), 0.1),
        'p_prompt': nrm(ks[5], (DEPTH, BATCH, SEQ, PLE_DIM), 1.0),
        'p_sample': nrm(ks[6], (DEPTH, DEC_BATCH, DEC_SEQ, PLE_DIM), 1.0),
        'g_mix': gain(ks[7], (DEPTH, D_MODEL)),
        'w_in': nrm(ks[8], (DEPTH, D_MODEL, IN_W), D_MODEL ** -0.5),
        'g_q': gain(ks[9], (DEPTH, Q_LORA)),
        'w_uq': nrm(ks[10], (DEPTH, Q_LORA, MLA_HEADS, QK_HEAD), Q_LORA ** -0.5),
        'g_kv': gain(ks[11], (DEPTH, KV_LORA)),
        'w_uk': nrm(ks[12], (DEPTH, KV_LORA, MLA_HEADS, NOPE_DIM), KV_LORA ** -0.5),
        'w_uv': nrm(ks[13], (DEPTH, KV_LORA, MLA_HEADS, V_HEAD), KV_LORA ** -0.5),
        'g_ret_out': gain(ks[14], (DEPTH, RET_W)),
        'g_mla_out': gain(ks[15], (DEPTH, MLA_W)),
        'w_out': nrm(ks[16], (DEPTH, MIX_W, D_MODEL), MIX_W ** -0.5),
        'g_ffn': gain(ks[17], (DEPTH, D_MODEL)),
        'w_pq': nrm(ks[18], (DEPTH, D_MODEL, PEER_HEADS * PEER_QDIM), D_MODEL ** -0.5),
        'sub_keys': nrm(ks[19], (DEPTH, PEER_HEADS, 2, N_KEYS, PEER_HALF), PEER_HALF ** -0.5),
        'u_tab': nrm(ks[20], (DEPTH, N_EXPERTS, D_MODEL), D_MODEL ** -0.5),
        'v_tab': nrm(ks[21], (DEPTH, N_EXPERTS, D_MODEL), 0.05),
        'g_ple': gain(ks[22], (DEPTH, D_MODEL)),
        'w_ple_gate': nrm(ks[23], (DEPTH, D_MODEL, D_MODEL), D_MODEL ** -0.5),
        'w_ple_proj': nrm(ks[24], (DEPTH, PLE_DIM, D_MODEL), PLE_DIM ** -0.5),
        'g_final': gain(ks[25], (D_MODEL,)),
    }


def reference(x_prompt, x_sample, cache_ckv, cache_krope, state_ret, p_prompt, p_sample,
              g_mix, w_in, g_q, w_uq, g_kv, w_uk, w_uv, g_ret_out, g_mla_out, w_out,
              g_ffn, w_pq, sub_keys, u_tab, v_tab, g_ple, w_ple_gate, w_ple_proj, g_final):
    Bp, Sp, _ = x_prompt.shape
    Bs, Ts, _ = x_sample.shape
    P = cache_ckv.shape[2]
    pos_p = jnp.arange(Sp, dtype=jnp.int32)
    pos_s = P + jnp.arange(Ts, dtype=jnp.int32)
    hp, hs = x_prompt, x_sample
    ckv_p, kr_p, ret_p, ckv_s, kr_s, ret_s = [], [], [], [], [], []
    for i in range(DEPTH):
        w = (g_mix[i], w_in[i], g_q[i], w_uq[i], g_kv[i], w_uk[i], w_uv[i], g_ret_out[i],
             g_mla_out[i], w_out[i], g_ffn[i], w_pq[i], sub_keys[i], u_tab[i], v_tab[i],
             g_ple[i], w_ple_gate[i], w_ple_proj[i])
        s0 = jnp.zeros((Bp, RET_HEADS, RET_DK, RET_DV), jnp.float32)
        hp, c1, k1, r1 = layer(hp, p_prompt[i], pos_p, s0, None, w)
        hs, c2, k2, r2 = layer(hs, p_sample[i], pos_s, state_ret[i], (cache_ckv[i], cache_krope[i]), w)
        ckv_p.append(c1); kr_p.append(k1); ret_p.append(r1)
        ckv_s.append(c2); kr_s.append(k2); ret_s.append(r2)
    y_prompt = rmsnorm(hp, g_final)
    y_sample = rmsnorm(hs, g_final)
    return (y_prompt, y_sample, jnp.stack(ckv_p), jnp.stack(kr_p), jnp.stack(ret_p),
            jnp.stack(ckv_s), jnp.stack(kr_s), jnp.stack(ret_s))
```

```python
import math
from contextlib import ExitStack
import numpy as np
import concourse.bass as bass
import concourse.mybir as mybir
from concourse.bass_utils import run_bass_kernel_spmd

F32 = mybir.dt.float32
BF16 = mybir.dt.bfloat16
I32 = mybir.dt.int32
U32 = mybir.dt.uint32
AF = mybir.ActivationFunctionType
ALU = mybir.AluOpType

NCORES = 8
D = 2048
NT = 9
NTOK = 1152
NPT = 8
SEQ = 8192
PAST = 4096
IN_W = 5184
OFF_RQ, OFF_RK, OFF_RV, OFF_RG, OFF_CQ, OFF_CKV, OFF_KR = 0, 1024, 2048, 3072, 4096, 4608, 5120
EPS = 1e-6
NREM = 56
SCALE = 192 ** -0.5

COMPUTE = ('pe', 'act', 'dve', 'pool')
NDSEM = 6


class Prog:
    def __init__(self, nc):
        self.nc = nc
        self.ops = {e: [] for e in ('pe', 'act', 'dve', 'pool', 'sp')}
        self.cnt = {e: 0 for e in COMPUTE}
        self.epoch = 0
        self.dcnt = {}
        self.drr = {q: 0 for q in ('sp', 'act', 'pool')}
        self.known = {e: {} for e in self.ops}
        self.pending = {e: [] for e in self.ops}
        self.last_w = {}
        self.readers = {}
        self.semh = {}

    def _sem(self, key):
        if key not in self.semh:
            name = 's_' + '_'.join(str(k) for k in (key if isinstance(key, tuple) else (key,)))
            self.semh[key] = self.nc.alloc_semaphore(name=name)
        return self.semh[key]

    def _deps(self, eng, reads, writes):
        deps = []
        for r in reads:
            if r in self.last_w:
                deps.append(self.last_w[r])
        for w in writes:
            if w in self.last_w:
                deps.append(self.last_w[w])
            deps.extend(self.readers.get(w, ()))
        waits = {}
        for (key, val) in deps:
            if key[0] == 'c' and key[1] == 'pe' and eng == 'pe':
                continue
            if self.known[eng].get(key, 0) >= val:
                continue
            if waits.get(key, 0) < val:
                waits[key] = val
        for key, val in waits.items():
            self.known[eng][key] = val
        out = self.pending[eng] + list(waits.items())
        self.pending[eng] = []
        return out

    def _record(self, token, reads, writes):
        for r in reads:
            self.readers.setdefault(r, []).append(token)
        for w in writes:
            self.last_w[w] = token
            self.readers[w] = []

    def op(self, eng, fn, reads=(), writes=()):
        waits = self._deps(eng, reads, writes)
        self.cnt[eng] += 1
        key = ('c', eng, self.epoch)
        token = (key, self.cnt[eng])
        self.ops[eng].append((waits, fn, (key, 1)))
        self._record(token, reads, writes)
        return token

    def dma(self, q, fn, reads=(), writes=()):
        j = self.drr[q]
        self.drr[q] = (j + 1) % NDSEM
        key = ('d', q, j)
        prev = self.dcnt.get(key, 0)
        waits = self._deps(q, reads, writes)
        if prev and self.known[q].get(key, 0) < prev:
            waits.append((key, prev))
            self.known[q][key] = prev
        self.dcnt[key] = prev + 16
        token = (key, prev + 16)
        self.ops[q].append((waits, fn, (key, 16)))
        self._record(token, reads, writes)
        return token

    def cc(self, fn, reads=(), writes=()):
        key = ('cc',)
        prev = self.dcnt.get(key, 0)
        waits = self._deps('pool', reads, writes)
        self.dcnt[key] = prev + 1
        token = (key, prev + 1)
        self.ops['pool'].append((waits, fn, (key, 1)))
        self._record(token, reads, writes)
        return token

    def _all_tokens(self):
        fin = [(('c', e, self.epoch), self.cnt[e]) for e in COMPUTE if self.cnt[e]]
        fin += list(self.dcnt.items())
        return fin

    def barrier(self):
        fin = self._all_tokens()
        for e in self.ops:
            for (k, v) in fin:
                if self.known[e].get(k, 0) < v:
                    self.pending[e] = [(kk, vv) for (kk, vv) in self.pending[e] if kk != k]
                    self.pending[e].append((k, v))
                    self.known[e][k] = v
        self.last_w.clear()
        self.readers.clear()
        self.epoch += 1
        self.cnt = {e: 0 for e in COMPUTE}

    def flush(self, final=False):
        nc = self.nc
        if final:
            self.barrier()
        ops = self.ops
        pend = self.pending if final else None
        sems = self._sem

        def run(engname, eng):
            for (waits, fn, inc) in ops[engname]:
                for (k, v) in waits:
                    eng.wait_ge(sems(k), v)
                ins = fn(eng)
                ins.then_inc(sems(inc[0]), inc[1])
            if pend is not None:
                for (k, v) in pend[engname]:
                    eng.wait_ge(sems(k), v)

        with nc.Block() as block:
            @block.sync
            def _(e):
                run('sp', e)

            @block.tensor
            def _(e):
                run('pe', e)

            @block.scalar
            def _(e):
                run('act', e)

            @block.vector
            def _(e):
                run('dve', e)

            @block.gpsimd
            def _(e):
                run('pool', e)
        self.ops = {e: [] for e in ops}
        if final:
            self.pending = {e: [] for e in ops}


class Arena:
    ESZ = {F32: 4, BF16: 2, I32: 4, U32: 4}

    def __init__(self, nc, nbytes):
        self.t = nc.alloc_sbuf_tensor("arena", [128, nbytes // 4], F32)
        self.off = 0
        self.cap = nbytes
        self.peak = 0

    def alloc(self, shape, dt=F32):
        esz = self.ESZ[dt]
        n = 1
        for d in shape[1:]:
            n *= d
        nb = (n * esz + 31) // 32 * 32
        off = self.off
        self.off += nb
        self.peak = max(self.peak, self.off)
        assert self.off <= self.cap, ("SBUF arena overflow", self.off, self.cap)
        ap = self.t[0:shape[0], off // 4:(off + nb) // 4]
        if dt != F32:
            ap = ap.bitcast(dt)
        ap = ap[:, 0:n]
        if len(shape) == 3:
            ap = ap.rearrange("p (a b) -> p a b", a=shape[1])
        elif len(shape) == 4:
            ap = ap.rearrange("p (a b c) -> p a b c", a=shape[1], b=shape[2])
        return ap

    def mark(self):
        return self.off

    def release(self, m):
        self.off = m


INPUT_SPECS = [
    ("x_all", [NTOK, D], F32), ("p_all", [NTOK, 256], F32), ("x_full", [NREM * 128, D], F32),
    ("rtab", [NREM, 128, 320], F32), ("wtab", [128, NREM, 4], F32),
    ("w_in", [D, IN_W], F32), ("w_uq", [512, 1536], F32), ("w_uk", [512, 1024], F32), ("w_uv", [512, 1024], F32),
    ("w_out", [D, D], F32), ("w_pq", [D, D], F32), ("subk", [16, 128, 128], F32),
    ("u_tab", [16384, D], F32), ("v_tab", [16384, D], F32), ("w_pg", [D, D], F32), ("w_pp", [256, D], F32),
    ("gpc_mix", [128, 16], F32), ("gpc_ffn", [128, 16], F32), ("gpc_ple", [128, 16], F32),
    ("gpc_q", [128, 4], F32), ("gpc_mla", [128, 8], F32),
    ("gb_kv", [128, 512], F32), ("gb_ret", [128, 1024], F32), ("gb_fin", [128, D], F32), ("gb_ffn", [128, D], F32),
    ("cache_ckv", [4, PAST, 512], F32), ("cache_kr", [4, PAST, 64], F32), ("state", [4, 4, 256, 256], F32),
    ("cosT128", [128, NTOK], F32), ("sinT128", [128, NTOK], F32),
    ("c64T", [64, NTOK], F32), ("s64T", [64, NTOK], F32),
    ("cos_tok", [128, NT, 32], F32), ("sin_tok", [128, NT, 32], F32),
    ("dmaskT", [64, 4, 64], F32), ("inner", [64, 4], F32), ("kdec64", [64, 4], F32), ("kdec32", [64, 4], F32),
    ("validcol", [128, 64], F32), ("maskblk", [128, 4, 512], F32),
    ("ident", [128, 128], F32), ("iota", [128, 128], F32),
]


def build_program(stop_after=99, dbg=(), small=()):
    nc = bass.Bass("TRN2", target_bir_lowering=False)
    I = {}
    for name, shape, dt in INPUT_SPECS:
        if name in small:
            shape = [min(shape[0], 128)] + list(shape[1:])
        I[name] = nc.dram_tensor(name, list(shape), dt, kind="ExternalInput").ap()
    O = {}
    for name, shape in [("y_all", [NTOK, D]), ("ckv_all", [NTOK, 512]), ("kr_all", [NTOK, 64]),
                        ("ret_s", [4, 4, 256, 256]), ("ret_p", [4, 256, 256])]:
        O[name] = nc.dram_tensor(name, list(shape), F32, kind="ExternalOutput").ap()
    DBG = {}

    def dbg_out(name, shape, dt=F32):
        DBG[name] = nc.dram_tensor("dbg_" + name, list(shape), dt, kind="ExternalOutput").ap()
        return DBG[name]

    vg_d = nc.dram_tensor("vg_d", [NTOK, 2048], BF16).ap()
    crem_d = nc.dram_tensor("crem_d", [128, 4, NREM * 128], BF16).ap()
    krem_d = nc.dram_tensor("krem_d", [64, NREM * 128], BF16).ap()
    mixT_d = nc.dram_tensor("mixT_d", [16, 128, NTOK], BF16).ap()
    q_d = nc.dram_tensor("q_d", [8, 192, NTOK], BF16).ap()
    osm_d = nc.dram_tensor("osm_d", [128, 1024], F32).ap()
    m_d = nc.dram_tensor("m_d", [NTOK, D], BF16).ap()
    own_d = nc.dram_tensor("own_d", [128, 512], BF16).ap()

    p = Prog(nc)
    P = p
    arena = Arena(nc, 206 * 1024)

    def SBg(name, shape, dt=F32):
        return arena.alloc(list(shape), dt)

    psA = nc.alloc_psum_tensor("psA", [128, 512], F32)
    psB = nc.alloc_psum_tensor("psB", [128, 512], F32)
    psC = nc.alloc_psum_tensor("psC", [128, 512], F32)
    psD = nc.alloc_psum_tensor("psD", [128, 512], F32)
    psO = nc.alloc_psum_tensor("psO", [128, 4, 512], F32)
    PS = {'A': psA, 'B': psB, 'C': psC, 'D': psD}

    def psbf(t):
        return t[:].bitcast(BF16)

    ident_f = SBg("ident_f", [128, 128], F32)
    ident_b = SBg("ident_b", [128, 128], BF16)
    P.dma('sp', lambda e: e.dma_start(out=ident_f[:], in_=I["ident"]), writes=['ident_f'])
    P.op('dve', lambda e: e.tensor_copy(out=ident_b[:], in_=ident_f[:]), reads=['ident_f'], writes=['ident_b'])

    rr = {'n': 0}
    cur_ph = {}

    def finish():
        P.flush(final=True)
        print("arena peak bytes/partition:", arena.peak, "ops:", dict(P.cnt), "dma sems:", {str(k): v for k, v in P.dcnt.items()})
        return nc, DBG

    def alt(a, b):
        rr['n'] += 1
        return a if rr['n'] % 2 else b

    def rstd_from_ss(ss_ap, rstd_ap, n, res_ss, res_rstd):
        P.op('act', lambda e: e.activation(out=rstd_ap, in_=ss_ap, func=AF.Sqrt, bias=EPS, scale=1.0 / n),
             reads=[res_ss], writes=[res_rstd])
        P.op('dve', lambda e: e.reciprocal(out=rstd_ap, in_=rstd_ap), reads=[res_rstd], writes=[res_rstd])

    def transpose_to(dst_fn, src_ap_fn, nchunks, src_res, dst_res, gain=None, npart=128, kpart=128, pbank='D', gain_res=None):
        c0 = 0
        while c0 < nchunks:
            n = min(8, nchunks - c0)
            pst = PS[pbank]
            pv = psbf(pst)
            for j in range(n):
                P.op('pe', lambda e, j=j, c=c0 + j: e.transpose(pv[0:kpart, j * 128:j * 128 + npart], src_ap_fn(c), ident_b[0:npart, 0:npart]),
                     reads=[src_res, 'ident_b'], writes=['ps' + pbank])
            for j in range(n):
                c = c0 + j
                if gain is not None:
                    P.op('dve', lambda e, j=j, c=c: e.tensor_scalar(out=dst_fn(c), in0=pv[0:kpart, j * 128:j * 128 + npart],
                                                                     scalar1=gain[0:kpart, c:c + 1], scalar2=None, op0=ALU.mult),
                         reads=['ps' + pbank, gain_res], writes=[dst_res])
                else:
                    P.op('act', lambda e, j=j, c=c: e.copy(out=dst_fn(c), in_=pv[0:kpart, j * 128:j * 128 + npart]),
                         reads=['ps' + pbank], writes=[dst_res])
            c0 += n

    def load_w_tile(wt, w_ap, c0, ncols, res, nk=16):
        src = w_ap.rearrange("(c p) n -> p c n", p=128)[:, :, c0:c0 + ncols]
        P.dma('pool', lambda e: e.dma_start(out=wt[:, 0:nk, 0:ncols], in_=src), writes=[res])

    markG = arena.mark()
    S32 = SBg("S32", [128, 4, 2, 256])
    markR = arena.mark()
    wR = SBg("wR", [128, 16, 2624], BF16)
    w3 = I["w_in"].rearrange("(c p) n -> p c n", p=128)
    P.dma('pool', lambda e: e.dma_start(out=wR[:, :, 0:1024], in_=w3[:, :, OFF_RK:OFF_RK + 1024]), writes=['wR0'])
    P.dma('pool', lambda e: e.dma_start(out=wR[:, :, 1024:2048], in_=w3[:, :, OFF_RV:OFF_RV + 1024]), writes=['wR1'])
    P.dma('pool', lambda e: e.dma_start(out=wR[:, :, 2048:2624], in_=w3[:, :, OFF_CKV:OFF_CKV + 576]), writes=['wR2'])
    wtab = SBg("wtab", [128, NREM, 4])
    gpcR = SBg("gpcR", [128, 16])
    gbkvR = SBg("gbkvR", [128, 512])
    P.dma('sp', lambda e: e.dma_start(out=wtab[:], in_=I["wtab"]), writes=['wtab'])
    P.dma('sp', lambda e: e.dma_start(out=gpcR[:], in_=I["gpc_mix"]), writes=['gpcR'])
    P.dma('sp', lambda e: e.dma_start(out=gbkvR[:], in_=I["gb_kv"]), writes=['gbkvR'])
    xR = [SBg("xR0", [128, D]), SBg("xR1", [128, D])]
    xsR = [SBg("xsR0", [128, D], BF16), SBg("xsR1", [128, D], BF16)]
    aTR = [SBg("aTR0", [128, 16, 128], BF16), SBg("aTR1", [128, 16, 128], BF16)]
    rtR = [SBg("rtR0", [128, 320]), SBg("rtR1", [128, 320])]
    junkR = SBg("junkR", [128, D], BF16)
    statR = SBg("statR", [128, 4 * NREM])
    P.op('pool', lambda e: e.memset(statR[:], 0.0), writes=['statR'])
    kt = [SBg("ktR%d" % i, [128, 2, 128]) for i in range(4)]
    krf = SBg("krf", [128, 4, 2, 128])
    kwb = [SBg("kwb0", [128, 4, 256], BF16), SBg("kwb1", [128, 4, 256], BF16)]
    vbR = [SBg("vbR0", [128, 1024], BF16), SBg("vbR1", [128, 1024], BF16)]
    cnb = SBg("cnb", [128, 512], BF16)
    cstg = [SBg("cstg0", [128, 4, 128], BF16), SBg("cstg1", [128, 4, 128], BF16)]
    krt_r = SBg("krt_r", [128, 4, 32])
    krbR = SBg("krbR", [128, 64], BF16)
    kstg = [SBg("kstg0", [64, 128], BF16), SBg("kstg1", [64, 128], BF16)]
    for g in range(NREM):
        b = g % 2
        X, XS, AT, RT = xR[b], xsR[b], aTR[b], rtR[b]
        rX, rXS, rAT, rRT = 'xR%d' % b, 'xsR%d' % b, 'aTR%d' % b, 'rtR%d' % b
        P.dma('sp', lambda e, X=X, g=g: e.dma_start(out=X[:], in_=I["x_full"][g * 128:(g + 1) * 128, :]), writes=[rX])
        P.dma('sp', lambda e, RT=RT, g=g: e.dma_start(out=RT[:], in_=I["rtab"][g]), writes=[rRT])
        P.op('act', lambda e, X=X, g=g: e.activation(out=junkR[:], in_=X[:], func=AF.Square, accum_out=statR[:, g:g + 1]),
             reads=[rX, 'statR'], writes=['junkR', 'ssR%d' % g])
        rstd_from_ss(statR[:, g:g + 1], statR[:, NREM + g:NREM + g + 1], D, 'ssR%d' % g, 'rsR%d' % g)
        P.op('dve', lambda e, X=X, XS=XS, g=g: e.tensor_scalar(out=XS[:], in0=X[:], scalar1=statR[:, NREM + g:NREM + g + 1],
                                                                scalar2=None, op0=ALU.mult), reads=[rX, 'rsR%d' % g], writes=[rXS])
        transpose_to(lambda c, AT=AT: AT[:, c, :], lambda c, XS=XS: XS[:, c * 128:(c + 1) * 128], 16, rXS, rAT, gain=gpcR, pbank='D', gain_res='gpcR')
        for half, pbk in ((0, 'A'), (1, 'B')):
            for c in range(16):
                P.op('pe', lambda e, c=c, half=half, pbk=pbk, AT=AT: e.matmul(PS[pbk][:, :], lhsT=AT[:, c, :], rhs=wR[:, c, half * 512:(half + 1) * 512],
                                                                             start=(c == 0), stop=(c == 15)), reads=[rAT, 'wR0'], writes=['ps' + pbk])
        cosb = RT[:, 0:128].unsqueeze(1).to_broadcast([128, 2, 128])
        sinb = RT[:, 128:256].unsqueeze(1).to_broadcast([128, 2, 128])
        for half, pbk in ((0, 'A'), (1, 'B')):
            v4 = PS[pbk][:, :].rearrange("p (h t f) -> p h t f", h=2, t=2)
            x1, x2 = v4[:, :, 0, :], v4[:, :, 1, :]
            P.op('dve', lambda e, x1=x1, cosb=cosb: e.tensor_tensor(out=kt[0][:], in0=x1, in1=cosb, op=ALU.mult), reads=['ps' + pbk, rRT], writes=['ktR0'])
            P.op('dve', lambda e, x2=x2, sinb=sinb: e.tensor_tensor(out=kt[1][:], in0=x2, in1=sinb, op=ALU.mult), reads=['ps' + pbk, rRT], writes=['ktR1'])
            P.op('dve', lambda e, x2=x2, cosb=cosb: e.tensor_tensor(out=kt[2][:], in0=x2, in1=cosb, op=ALU.mult), reads=['ps' + pbk, rRT], writes=['ktR2'])
            P.op('dve', lambda e, x1=x1, sinb=sinb: e.tensor_tensor(out=kt[3][:], in0=x1, in1=sinb, op=ALU.mult), reads=['ps' + pbk, rRT], writes=['ktR3'])
            P.op('pool', lambda e, half=half: e.tensor_tensor(out=krf[:, 2 * half:2 * half + 2, 0, :], in0=kt[0][:], in1=kt[1][:], op=ALU.subtract),
                 reads=['ktR0', 'ktR1'], writes=['krf'])
            P.op('pool', lambda e, half=half: e.tensor_tensor(out=krf[:, 2 * half:2 * half + 2, 1, :], in0=kt[2][:], in1=kt[3][:], op=ALU.add),
                 reads=['ktR2', 'ktR3'], writes=['krf'])
        KW, rKW = kwb[b], 'kwb%d' % b
        for h in range(4):
            P.op('pool', lambda e, h=h, g=g, KW=KW: e.tensor_scalar(out=KW[:, h, :], in0=krf[:, h, :, :], scalar1=wtab[:, g, h:h + 1], scalar2=None, op0=ALU.mult),
                 reads=['krf', 'wtab'], writes=[rKW])
        VB, rVB = vbR[b], 'vbR%d' % b
        for half, pbk in ((0, 'A'), (1, 'B')):
            for c in range(16):
                P.op('pe', lambda e, c=c, half=half, pbk=pbk, AT=AT: e.matmul(PS[pbk][:, :], lhsT=AT[:, c, :], rhs=wR[:, c, 1024 + half * 512:1024 + (half + 1) * 512],
                                                                             start=(c == 0), stop=(c == 15)), reads=[rAT, 'wR1'], writes=['ps' + pbk])
            P.op('act', lambda e, half=half, pbk=pbk, VB=VB: e.copy(out=VB[:, half * 512:(half + 1) * 512], in_=PS[pbk][:, :]), reads=['ps' + pbk], writes=[rVB])
        for c in range(16):
            P.op('pe', lambda e, c=c, AT=AT: e.matmul(psC[:, :], lhsT=AT[:, c, :], rhs=wR[:, c, 2048:2560], start=(c == 0), stop=(c == 15)),
                 reads=[rAT, 'wR2'], writes=['psC'])
        P.op('act', lambda e, g=g: e.activation(out=junkR[:, 0:512], in_=psC[:, :], func=AF.Square, accum_out=statR[:, 2 * NREM + g:2 * NREM + g + 1]),
             reads=['psC', 'statR'], writes=['junkR', 'scR%d' % g])
        rstd_from_ss(statR[:, 2 * NREM + g:2 * NREM + g + 1], statR[:, 3 * NREM + g:3 * NREM + g + 1], 512, 'scR%d' % g, 'rcR%d' % g)
        P.op('dve', lambda e, g=g: e.scalar_tensor_tensor(out=cnb[:], in0=psC[:, :], scalar=statR[:, 3 * NREM + g:3 * NREM + g + 1], in1=gbkvR[:],
                                                           op0=ALU.mult, op1=ALU.mult), reads=['psC', 'rcR%d' % g, 'gbkvR'], writes=['cnb'])
        CS, rCS = cstg[b], 'cstg%d' % b
        transpose_to(lambda c, CS=CS: CS[:, c, :], lambda c: cnb[:, c * 128:(c + 1) * 128], 4, 'cnb', rCS, gain=None, pbank='C')
        P.dma('sp', lambda e, CS=CS, g=g: e.dma_start(out=crem_d[:, :, g * 128:(g + 1) * 128], in_=CS[:]), reads=[rCS], writes=['crem_%d' % g])
        for c in range(16):
            P.op('pe', lambda e, c=c, AT=AT: e.matmul(psD[:, 0:64], lhsT=AT[:, c, :], rhs=wR[:, c, 2560:2624], start=(c == 0), stop=(c == 15)),
                 reads=[rAT, 'wR2'], writes=['psD'])
        x1, x2 = psD[:, 0:32], psD[:, 32:64]
        cs, sn = RT[:, 256:288], RT[:, 288:320]
        P.op('dve', lambda e, x1=x1, cs=cs: e.tensor_tensor(out=krt_r[:, 0, :], in0=x1, in1=cs, op=ALU.mult), reads=['psD', rRT], writes=['krtr0'])
        P.op('dve', lambda e, x2=x2, sn=sn: e.tensor_tensor(out=krt_r[:, 1, :], in0=x2, in1=sn, op=ALU.mult), reads=['psD', rRT], writes=['krtr1'])
        P.op('dve', lambda e, x2=x2, cs=cs: e.tensor_tensor(out=krt_r[:, 2, :], in0=x2, in1=cs, op=ALU.mult), reads=['psD', rRT], writes=['krtr2'])
        P.op('dve', lambda e, x1=x1, sn=sn: e.tensor_tensor(out=krt_r[:, 3, :], in0=x1, in1=sn, op=ALU.mult), reads=['psD', rRT], writes=['krtr3'])
        P.op('pool', lambda e: e.tensor_tensor(out=krbR[:, 0:32], in0=krt_r[:, 0, :], in1=krt_r[:, 1, :], op=ALU.subtract), reads=['krtr0', 'krtr1'], writes=['krbR'])
        P.op('pool', lambda e: e.tensor_tensor(out=krbR[:, 32:64], in0=krt_r[:, 2, :], in1=krt_r[:, 3, :], op=ALU.add), reads=['krtr2', 'krtr3'], writes=['krbR'])
        KS, rKS = kstg[b], 'kstg%d' % b
        transpose_to(lambda c, KS=KS: KS[:, :], lambda c: krbR[:, 0:64], 1, 'krbR', rKS, gain=None, kpart=64, pbank='D')
        P.dma('sp', lambda e, KS=KS, g=g: e.dma_start(out=krem_d[:, g * 128:(g + 1) * 128], in_=KS[:]), reads=[rKS], writes=['krem_%d' % g])
        for h in range(4):
            for dc in range(2):
                P.op('pe', lambda e, h=h, dc=dc, g=g, KW=KW, VB=VB: e.matmul(psO[:, h, dc * 256:(dc + 1) * 256], lhsT=KW[:, h, dc * 128:(dc + 1) * 128],
                                                                               rhs=VB[:, h * 256:(h + 1) * 256], start=(g == 0 and dc == 0), stop=(g == NREM - 1 and dc == 1)),
                     reads=[rKW, rVB], writes=['psO'])
    for h in range(4):
        P.op('act' if h % 2 else 'dve', (lambda e, h=h: e.copy(out=S32[:, h, :, :], in_=psO[:, h, :].rearrange("p (c e) -> p c e", c=2))) if h % 2 else
             (lambda e, h=h: e.tensor_copy(out=S32[:, h, :, :], in_=psO[:, h, :].rearrange("p (c e) -> p c e", c=2))),
             reads=['psO'], writes=['S32_%d' % h])
    if 'S0' in dbg:
        d = dbg_out('S0', [128, 4, 2, 256])
        P.dma('sp', lambda e, d=d: e.dma_start(out=d, in_=S32[:]), reads=['S32_%d' % h for h in range(4)])
    P.barrier()
    arena.release(markR)
    if stop_after == 0.5:
        return finish()

    SB1 = SBg

    cqT_all = SBg("cqT_all", [128, 4, NTOK], BF16)
    ckvT_own = SBg("ckvT_own", [128, 4, NTOK], BF16)
    krT_own = SBg("krT_own", [64, NTOK], BF16)
    mark12 = arena.mark()
    qT_all = SBg("qT_all", [128, 8, NTOK], BF16)
    kT_all = SBg("kT_all", [128, 8, NTOK], BF16)

    mark1 = arena.mark()
    aT_all = SB1("aT_all", [128, 16, NTOK], BF16)
    gpc_mix = SB1("gpc_mix", [128, 16])
    gpc_q = SB1("gpc_q", [128, 4])
    gb_kv = SB1("gb_kv", [128, 512])
    cosT = SB1("cosT", [128, NTOK])
    sinT = SB1("sinT", [128, NTOK])
    cos_tok = SB1("cos_tok", [128, NT, 32])
    sin_tok = SB1("sin_tok", [128, NT, 32])
    for nm, t, rn in [("gpc_mix", gpc_mix, 'gpc_mix'), ("gpc_q", gpc_q, 'gpc_q'), ("gb_kv", gb_kv, 'gb_kv'),
                      ("cosT128", cosT, 'cosT'), ("sinT128", sinT, 'sinT'), ("cos_tok", cos_tok, 'cos_tok'), ("sin_tok", sin_tok, 'sin_tok')]:
        P.dma('sp', lambda e, nm=nm, t=t: e.dma_start(out=t[:], in_=I[nm]), writes=[rn])
    cres = ['gpc_mix', 'gpc_q', 'gb_kv', 'cosT', 'sinT', 'cos_tok', 'sin_tok']
    xt = [SB1("xt0", [128, D]), SB1("xt1", [128, D])]
    xs = [SB1("xs0", [128, D], BF16), SB1("xs1", [128, D], BF16)]
    junk = SB1("junk", [128, D], BF16)
    stat = SB1("stat", [128, 4 * NT + 8])
    P.op('pool', lambda e: e.memset(stat[:], 0.0), writes=['ss1_%d' % t for t in range(NT)] + ['ssz%d' % i for i in range(2 * NT, 4 * NT)])
    wts = [SB1("wt0", [128, 16, 512], BF16), SB1("wt1", [128, 16, 512], BF16)]

    for t in range(NT):
        b = t % 2
        X, XS = xt[b], xs[b]
        P.dma('sp', lambda e, X=X, t=t: e.dma_start(out=X[:], in_=I["x_all"][t * 128:(t + 1) * 128, :]), writes=['xt%d' % b])
        P.op('act', lambda e, X=X, t=t: e.activation(out=junk[:], in_=X[:], func=AF.Square, accum_out=stat[:, t:t + 1]),
             reads=['xt%d' % b], writes=['junk', 'ss1_%d' % t])
        rstd_from_ss(stat[:, t:t + 1], stat[:, NT + t:NT + t + 1], D, 'ss1_%d' % t, 'rs1_%d' % t)
        P.op('dve', lambda e, X=X, XS=XS, t=t: e.tensor_scalar(out=XS[:], in0=X[:], scalar1=stat[:, NT + t:NT + t + 1],
                                                                  scalar2=None, op0=ALU.mult),
             reads=['xt%d' % b, 'rs1_%d' % t], writes=['xs%d' % b])
        transpose_to(lambda c, t=t: aT_all[:, c, t * 128:(t + 1) * 128], lambda c, XS=XS: XS[:, c * 128:(c + 1) * 128],
                     16, 'xs%d' % b, 'aT_%d' % t, gain=gpc_mix, pbank=alt('C', 'D'), gain_res='gpc_mix')
    aT_res = ['aT_%d' % t for t in range(NT)]

    if 'aT' in dbg:
        d = dbg_out('aT', [128, 16, NTOK], BF16)
        P.dma('sp', lambda e, d=d: e.dma_start(out=d, in_=aT_all[:]), reads=aT_res)

    if stop_after == 1.1:
        return finish()
    tokblks = [(0, 512), (512, 512), (1024, 128)]
    rt = [SB1("rt%d" % i, [128, 512]) for i in range(4)]
    wi = 0
    for qk, (off, dstT) in enumerate([(OFF_RQ, qT_all), (OFF_RK, kT_all)]):
        for hp in range(2):
            wt = wts[wi % 2]
            wres = 'wt%d' % (wi % 2)
            wi += 1
            load_w_tile(wt, I["w_in"], off + hp * 512, 512, wres)
            for hh in range(2):
                h = hp * 2 + hh
                for (n0, nn) in tokblks:
                    for half, pbk in ((0, 'A'), (1, 'B')):
                        cb = hh * 256 + half * 128
                        for c in range(16):
                            P.op('pe', lambda e, c=c, cb=cb, pbk=pbk, wt=wt, n0=n0, nn=nn: e.matmul(
                                PS[pbk][:, 0:nn], lhsT=wt[:, c, cb:cb + 128], rhs=aT_all[:, c, n0:n0 + nn],
                                start=(c == 0), stop=(c == 15)), reads=[wres] + aT_res, writes=['ps' + pbk])
                    cs, sn = cosT[:, n0:n0 + nn], sinT[:, n0:n0 + nn]
                    x1, x2 = psA[:, 0:nn], psB[:, 0:nn]
                    P.op('dve', lambda e, x1=x1, cs=cs, nn=nn: e.tensor_tensor(out=rt[0][:, 0:nn], in0=x1, in1=cs, op=ALU.mult),
                         reads=['psA', 'cosT'], writes=['rt0'])
                    P.op('dve', lambda e, x2=x2, sn=sn, nn=nn: e.tensor_tensor(out=rt[1][:, 0:nn], in0=x2, in1=sn, op=ALU.mult),
                         reads=['psB', 'sinT'], writes=['rt1'])
                    P.op('dve', lambda e, x2=x2, cs=cs, nn=nn: e.tensor_tensor(out=rt[2][:, 0:nn], in0=x2, in1=cs, op=ALU.mult),
                         reads=['psB', 'cosT'], writes=['rt2'])
                    P.op('dve', lambda e, x1=x1, sn=sn, nn=nn: e.tensor_tensor(out=rt[3][:, 0:nn], in0=x1, in1=sn, op=ALU.mult),
                         reads=['psA', 'sinT'], writes=['rt3'])
                    dres = ('qT' if qk == 0 else 'kT') + '_%d' % h
                    P.op('pool', lambda e, dstT=dstT, h=h, n0=n0, nn=nn: e.tensor_tensor(
                        out=dstT[:, 2 * h, n0:n0 + nn], in0=rt[0][:, 0:nn], in1=rt[1][:, 0:nn], op=ALU.subtract),
                        reads=['rt0', 'rt1'], writes=[dres])
                    P.op('pool', lambda e, dstT=dstT, h=h, n0=n0, nn=nn: e.tensor_tensor(
                        out=dstT[:, 2 * h + 1, n0:n0 + nn], in0=rt[2][:, 0:nn], in1=rt[3][:, 0:nn], op=ALU.add),
                        reads=['rt2', 'rt3'], writes=[dres])

    if 'qT' in dbg:
        d = dbg_out('qT', [128, 8, NTOK], BF16)
        P.dma('sp', lambda e, d=d: e.dma_start(out=d, in_=qT_all[:]), reads=['qT_%d' % h for h in range(4)])
        d2 = dbg_out('kT', [128, 8, NTOK], BF16)
        P.dma('sp', lambda e, d2=d2: e.dma_start(out=d2, in_=kT_all[:]), reads=['kT_%d' % h for h in range(4)])

    if stop_after == 1.2:
        return finish()
    vgs = [SB1("vgs0", [128, 512], BF16), SB1("vgs1", [128, 512], BF16)]
    vi = 0
    for ct in range(4):
        wt = wts[wi % 2]
        wres = 'wt%d' % (wi % 2)
        wi += 1
        load_w_tile(wt, I["w_in"], OFF_RV + ct * 512, 512, wres)
        for t in range(NT):
            pbk = alt('A', 'B')
            for c in range(16):
                P.op('pe', lambda e, c=c, t=t, pbk=pbk, wt=wt: e.matmul(
                    PS[pbk][:, :], lhsT=aT_all[:, c, t * 128:(t + 1) * 128], rhs=wt[:, c, :],
                    start=(c == 0), stop=(c == 15)), reads=[wres, 'aT_%d' % t], writes=['ps' + pbk])
            vb = vgs[vi % 2]
            vres = 'vgs%d' % (vi % 2)
            vi += 1
            if ct < 2:
                P.op('act', lambda e, vb=vb, pbk=pbk: e.copy(out=vb[:], in_=PS[pbk][:, :]), reads=['ps' + pbk], writes=[vres])
            else:
                P.op('act', lambda e, vb=vb, pbk=pbk: e.activation(out=vb[:], in_=PS[pbk][:, :], func=AF.Silu),
                     reads=['ps' + pbk], writes=[vres])
            P.dma('sp', lambda e, vb=vb, t=t, ct=ct: e.dma_start(out=vg_d[t * 128:(t + 1) * 128, ct * 512:(ct + 1) * 512], in_=vb[:]),
                  reads=[vres], writes=['vg_d_%d_%d' % (t, ct)])

    if stop_after == 1.3:
        return finish()
    zt = [SB1("zt0", [128, 512]), SB1("zt1", [128, 512])]
    zb = [SB1("zb0", [128, 512], BF16), SB1("zb1", [128, 512], BF16)]
    krs = [SB1("krs0", [128, 64]), SB1("krs1", [128, 64])]
    krt = SB1("krt", [128, 4, 32])
    krb = [SB1("krb0", [128, 64], BF16), SB1("krb1", [128, 64], BF16)]
    wkr = SB1("wkr", [128, 16, 64], BF16)
    zi = 0
    for part in range(2):
        wt = wts[wi % 2]
        wres = 'wt%d' % (wi % 2)
        wi += 1
        load_w_tile(wt, I["w_in"], OFF_CQ + part * 512, 512, wres)
        for t in range(NT):
            pbk = alt('A', 'B')
            for c in range(16):
                P.op('pe', lambda e, c=c, t=t, pbk=pbk, wt=wt: e.matmul(
                    PS[pbk][:, :], lhsT=aT_all[:, c, t * 128:(t + 1) * 128], rhs=wt[:, c, :],
                    start=(c == 0), stop=(c == 15)), reads=[wres, 'aT_%d' % t], writes=['ps' + pbk])
            si = 2 * NT + part * NT + t
            Z = zt[zi % 2]
            ZB = zb[zi % 2]
            zres, zbres = 'zt%d' % (zi % 2), 'zb%d' % (zi % 2)
            zi += 1
            P.op('act', lambda e, pbk=pbk, si=si: e.activation(out=junk[:, 0:512], in_=PS[pbk][:, :], func=AF.Square,
                                                                accum_out=stat[:, si:si + 1]),
                 reads=['ps' + pbk], writes=['junk', 'ssz%d' % si])
            rstd_from_ss(stat[:, si:si + 1], stat[:, si:si + 1], 512, 'ssz%d' % si, 'rsz%d' % si)
            if part == 0:
                P.op('dve', lambda e, ZB=ZB, pbk=pbk, si=si: e.tensor_scalar(out=ZB[:], in0=PS[pbk][:, :], scalar1=stat[:, si:si + 1],
                                                                              scalar2=None, op0=ALU.mult),
                     reads=['ps' + pbk, 'rsz%d' % si], writes=[zbres])
                transpose_to(lambda c, t=t: cqT_all[:, c, t * 128:(t + 1) * 128], lambda c, ZB=ZB: ZB[:, c * 128:(c + 1) * 128],
                             4, zbres, 'cqT_%d' % t, gain=gpc_q, pbank=alt('C', 'D'), gain_res='gpc_q')
            else:
                P.op('dve', lambda e, Z=Z, pbk=pbk, si=si: e.scalar_tensor_tensor(
                    out=Z[:], in0=PS[pbk][:, :], scalar=stat[:, si:si + 1], in1=gb_kv[:], op0=ALU.mult, op1=ALU.mult),
                    reads=['ps' + pbk, 'rsz%d' % si, 'gb_kv'], writes=[zres])
                P.dma('sp', lambda e, Z=Z, t=t: e.dma_start(out=O["ckv_all"][t * 128:(t + 1) * 128, :], in_=Z[:]),
                      reads=[zres], writes=['ckv_all_%d' % t])
                P.op('act', lambda e, Z=Z, ZB=ZB: e.copy(out=ZB[:], in_=Z[:]), reads=[zres], writes=[zbres])
                if t == NT - 1:
                    P.dma('sp', lambda e, ZB=ZB: e.dma_start(out=own_d, in_=ZB[:]), reads=[zbres], writes=['own_d'])
                transpose_to(lambda c, t=t: ckvT_own[:, c, t * 128:(t + 1) * 128], lambda c, ZB=ZB: ZB[:, c * 128:(c + 1) * 128],
                             4, zbres, 'ckvT_%d' % t, gain=None, pbank=alt('C', 'D'))
    if stop_after == 1.4:
        return finish()
    P.dma('pool', lambda e: e.dma_start(out=wkr[:], in_=I["w_in"].rearrange("(c p) n -> p c n", p=128)[:, :, OFF_KR:OFF_KR + 64]),
          writes=['wkr'])
    for t in range(NT):
        pbk = alt('A', 'B')
        for c in range(16):
            P.op('pe', lambda e, c=c, t=t, pbk=pbk: e.matmul(
                PS[pbk][:, 0:64], lhsT=aT_all[:, c, t * 128:(t + 1) * 128], rhs=wkr[:, c, :],
                start=(c == 0), stop=(c == 15)), reads=['wkr', 'aT_%d' % t], writes=['ps' + pbk])
        K = krs[t % 2]
        KB = krb[t % 2]
        kres, kbres = 'krs%d' % (t % 2), 'krb%d' % (t % 2)
        x1, x2 = PS[pbk][:, 0:32], PS[pbk][:, 32:64]
        cs, sn = cos_tok[:, t, :], sin_tok[:, t, :]
        P.op('dve', lambda e, x1=x1, cs=cs: e.tensor_tensor(out=krt[:, 0, :], in0=x1, in1=cs, op=ALU.mult), reads=['ps' + pbk, 'cos_tok'], writes=['krt0'])
        P.op('dve', lambda e, x2=x2, sn=sn: e.tensor_tensor(out=krt[:, 1, :], in0=x2, in1=sn, op=ALU.mult), reads=['ps' + pbk, 'sin_tok'], writes=['krt1'])
        P.op('dve', lambda e, x2=x2, cs=cs: e.tensor_tensor(out=krt[:, 2, :], in0=x2, in1=cs, op=ALU.mult), reads=['ps' + pbk, 'cos_tok'], writes=['krt2'])
        P.op('dve', lambda e, x1=x1, sn=sn: e.tensor_tensor(out=krt[:, 3, :], in0=x1, in1=sn, op=ALU.mult), reads=['ps' + pbk, 'sin_tok'], writes=['krt3'])
        P.op('dve', lambda e, K=K: e.tensor_tensor(out=K[:, 0:32], in0=krt[:, 0, :], in1=krt[:, 1, :], op=ALU.subtract),
             reads=['krt0', 'krt1'], writes=[kres])
        P.op('dve', lambda e, K=K: e.tensor_tensor(out=K[:, 32:64], in0=krt[:, 2, :], in1=krt[:, 3, :], op=ALU.add),
             reads=['krt2', 'krt3'], writes=[kres])
        P.dma('sp', lambda e, K=K, t=t: e.dma_start(out=O["kr_all"][t * 128:(t + 1) * 128, :], in_=K[:]), reads=[kres], writes=['kr_all_%d' % t])
        P.op('act', lambda e, K=K, KB=KB: e.copy(out=KB[:], in_=K[:]), reads=[kres], writes=[kbres])
        transpose_to(lambda c, t=t: krT_own[:, t * 128:(t + 1) * 128], lambda c, KB=KB: KB[:, 0:64],
                     1, kbres, 'krT_%d' % t, gain=None, kpart=64, pbank=alt('C', 'D'))
    if 'own' in dbg:
        d = dbg_out('cqT', [128, 4, NTOK], BF16)
        P.dma('sp', lambda e, d=d: e.dma_start(out=d, in_=cqT_all[:]), reads=['cqT_%d' % t for t in range(NT)])
        d = dbg_out('ckvT', [128, 4, NTOK], BF16)
        P.dma('sp', lambda e, d=d: e.dma_start(out=d, in_=ckvT_own[:]), reads=['ckvT_%d' % t for t in range(NT)])
        d = dbg_out('krT', [64, NTOK], BF16)
        P.dma('sp', lambda e, d=d: e.dma_start(out=d, in_=krT_own[:]), reads=['krT_%d' % t for t in range(NT)])
    P.barrier()
    arena.release(mark1)
    if stop_after <= 1:
        return finish()

    mark2 = arena.mark()
    CON = make_consts(0)
    gL64 = [float(v) for v in CON["gL64"]]
    gL32 = [float(v) for v in CON["gL32"]]
    dmaskT = SBg("dmaskT", [64, 4, 64])
    inner = SBg("inner", [64, 4])
    kdec64 = SBg("kdec64", [64, 4])
    kdec32 = SBg("kdec32", [64, 4])
    gb_ret = SBg("gb_ret", [64, 1024])
    for nm, t in [("dmaskT", dmaskT), ("inner", inner), ("kdec64", kdec64), ("kdec32", kdec32)]:
        P.dma('sp', lambda e, nm=nm, t=t: e.dma_start(out=t[:], in_=I[nm]), writes=[nm])
    P.dma('sp', lambda e: e.dma_start(out=gb_ret[:], in_=I["gb_ret"][0:64, :]), writes=['gb_ret'])
    Sbf = SBg("Sbf", [128, 4, 2, 256], BF16)
    vgcs = [SBg("vgc0", [64, 2048], BF16), SBg("vgc1", [64, 2048], BF16)]
    ktok = SBg("ktok", [64, 4, 256], BF16)
    PTs = [SBg("PT0", [64, 64], BF16), SBg("PT1", [64, 64], BF16)]
    o_c = SBg("o_c", [64, 4, 256])
    gst = SBg("gst", [64, 24])
    gjunk = SBg("gjunk", [64, 256], BF16)
    ynorm = SBg("ynorm", [64, 1024])
    omix = SBg("omix", [64, 1024], BF16)
    mixst = [SBg("mixst0", [128, 8, 64], BF16), SBg("mixst1", [128, 8, 64], BF16)]
    psets = [((psA, 'psA'), (psB, 'psB'), (psC, 'psC'), (psD, 'psD')),
             ((psO[:, 0, :], 'psO0'), (psO[:, 1, :], 'psO1'), (psO[:, 2, :], 'psO2'), (psO[:, 3, :], 'psO3'))]
    cstate = {'ci': 0}

    def ret_chunk(tok0, L, kdec, kdec_res, gL, compute_out):
        ci = cstate['ci']
        cstate['ci'] += 1
        vb = ci % 2
        vgc, vres = vgcs[vb], 'vgc%d' % vb
        P.dma('sp', lambda e: e.dma_start(out=vgc[0:L, :], in_=vg_d[tok0:tok0 + L, :]), writes=[vres])
        for h in range(4):
            st = h % 2
            (bS, rS), (bXY, rXY), (bD, rD), (bT, rT) = psets[st]
            pvT = bT.bitcast(BF16)
            for dc in range(2):
                P.op('pe', lambda e, dc=dc, h=h, pvT=pvT: e.transpose(pvT[0:L, dc * 128:(dc + 1) * 128], kT_all[:, 2 * h + dc, tok0:tok0 + L], ident_b[:, :]),
                     reads=['kT_%d' % h, 'ident_b'], writes=[rT])
            P.op('dve', lambda e, h=h, pvT=pvT: e.tensor_scalar(out=ktok[0:L, h, :], in0=pvT[0:L, 0:256], scalar1=kdec[0:L, h:h + 1],
                                                                 scalar2=None, op0=ALU.mult),
                 reads=[rT, kdec_res], writes=['ktok%d' % h])
            if compute_out:
                PT, rPT = PTs[st], 'PT%d' % st
                for dc in range(2):
                    P.op('pe', lambda e, dc=dc, h=h, bS=bS: e.matmul(bS[0:L, 0:L], lhsT=kT_all[:, 2 * h + dc, tok0:tok0 + L],
                                                                      rhs=qT_all[:, 2 * h + dc, tok0:tok0 + L], start=(dc == 0), stop=(dc == 1)),
                         reads=['kT_%d' % h, 'qT_%d' % h], writes=[rS])
                P.op('dve', lambda e, h=h, bS=bS, PT=PT: e.tensor_tensor(out=PT[0:L, 0:L], in0=bS[0:L, 0:L], in1=dmaskT[0:L, h, 0:L], op=ALU.mult),
                     reads=[rS, 'dmaskT'], writes=[rPT])
                P.op('pe', lambda e, h=h, bXY=bXY, PT=PT: e.matmul(bXY[0:L, 0:256], lhsT=PT[0:L, 0:L], rhs=vgc[0:L, h * 256:(h + 1) * 256],
                                                                    start=True, stop=True), reads=[rPT, vres], writes=[rXY])
                for dc in range(2):
                    P.op('pe', lambda e, dc=dc, h=h, bXY=bXY: e.matmul(bXY[0:L, 256:512], lhsT=qT_all[:, 2 * h + dc, tok0:tok0 + L],
                                                                        rhs=Sbf[:, h, dc, :], start=(dc == 0), stop=(dc == 1)),
                         reads=['qT_%d' % h, 'Sbf_%d' % h], writes=[rXY])
                P.op('act', lambda e, h=h, bXY=bXY: e.copy(out=o_c[0:L, h, :], in_=bXY[0:L, 0:256]), reads=[rXY], writes=['o_c%d' % h])
                P.op('dve', lambda e, h=h, bXY=bXY: e.scalar_tensor_tensor(out=o_c[0:L, h, :], in0=bXY[0:L, 256:512], scalar=inner[0:L, h:h + 1],
                                                                            in1=o_c[0:L, h, :], op0=ALU.mult, op1=ALU.add),
                     reads=[rXY, 'o_c%d' % h, 'inner'], writes=['o_c%d' % h])
            for dc in range(2):
                P.op('pe', lambda e, dc=dc, h=h, bD=bD: e.matmul(bD[:, dc * 256:(dc + 1) * 256], lhsT=ktok[0:L, h, dc * 128:(dc + 1) * 128],
                                                                  rhs=vgc[0:L, h * 256:(h + 1) * 256], start=True, stop=True),
                     reads=['ktok%d' % h, vres], writes=[rD])
            for dc in range(2):
                P.op('dve', lambda e, dc=dc, h=h, bD=bD: e.scalar_tensor_tensor(out=S32[:, h, dc, :], in0=S32[:, h, dc, :], scalar=gL[h],
                                                                                 in1=bD[:, dc * 256:(dc + 1) * 256], op0=ALU.mult, op1=ALU.add),
                     reads=[rD, 'S32_%d' % h], writes=['S32_%d' % h])
            if compute_out:
                P.op('act', lambda e, h=h: e.copy(out=Sbf[:, h, :, :], in_=S32[:, h, :, :]), reads=['S32_%d' % h], writes=['Sbf_%d' % h])
        if not compute_out:
            return
        for h in range(4):
            P.op('act', lambda e, h=h: e.activation(out=gjunk[0:L, :], in_=o_c[0:L, h, :], func=AF.Copy, accum_out=gst[0:L, h:h + 1]),
                 reads=['o_c%d' % h], writes=['gjunk', 'gs1'])
            P.op('act', lambda e, h=h: e.activation(out=gjunk[0:L, :], in_=o_c[0:L, h, :], func=AF.Square, accum_out=gst[0:L, 4 + h:5 + h]),
                 reads=['o_c%d' % h], writes=['gjunk', 'gs2'])
        P.op('dve', lambda e: e.tensor_scalar(out=gst[0:L, 8:12], in0=gst[0:L, 0:4], scalar1=1.0 / 256, scalar2=None, op0=ALU.mult),
             reads=['gs1'], writes=['gmean'])
        P.op('dve', lambda e: e.tensor_tensor(out=gst[0:L, 12:16], in0=gst[0:L, 8:12], in1=gst[0:L, 8:12], op=ALU.mult),
             reads=['gmean'], writes=['gmsq'])
        P.op('dve', lambda e: e.scalar_tensor_tensor(out=gst[0:L, 16:20], in0=gst[0:L, 4:8], scalar=1.0 / 256, in1=gst[0:L, 12:16],
                                                      op0=ALU.mult, op1=ALU.subtract), reads=['gs2', 'gmsq'], writes=['gvar'])
        P.op('act', lambda e: e.activation(out=gst[0:L, 20:24], in_=gst[0:L, 16:20], func=AF.Sqrt, bias=EPS, scale=1.0), reads=['gvar'], writes=['grstd'])
        P.op('dve', lambda e: e.reciprocal(out=gst[0:L, 20:24], in_=gst[0:L, 20:24]), reads=['grstd'], writes=['grstd'])
        for h in range(4):
            P.op('dve', lambda e, h=h: e.tensor_scalar(out=ynorm[0:L, h * 256:(h + 1) * 256], in0=o_c[0:L, h, :], scalar1=gst[0:L, 8 + h:9 + h],
                                                        scalar2=gst[0:L, 20 + h:21 + h], op0=ALU.subtract, op1=ALU.mult),
                 reads=['o_c%d' % h, 'gmean', 'grstd'], writes=['ynorm'])
        P.op('pool', lambda e: e.tensor_tensor(out=ynorm[0:L, :], in0=ynorm[0:L, :], in1=gb_ret[0:L, :], op=ALU.mult),
             reads=['ynorm', 'gb_ret'], writes=['ynorm'])
        P.op('pool', lambda e: e.tensor_tensor(out=omix[0:L, :], in0=ynorm[0:L, :], in1=vgc[0:L, 1024:2048], op=ALU.mult),
             reads=['ynorm', vres], writes=['omix'])
        (bT, rT) = psets[0][3]
        pvT = bT.bitcast(BF16)
        for j in range(8):
            P.op('pe', lambda e, j=j, pvT=pvT: e.transpose(pvT[:, j * 64:j * 64 + L], omix[0:L, j * 128:(j + 1) * 128], ident_b[0:L, 0:L]),
                 reads=['omix', 'ident_b'], writes=[rT])
        ms, rms_ = mixst[vb], 'mixst%d' % vb
        P.op('act', lambda e, pvT=pvT, ms=ms: e.copy(out=ms[:, :, 0:L], in_=pvT[:, 0:512].rearrange("p (j l) -> p j l", j=8)[:, :, 0:L]),
             reads=[rT], writes=[rms_])
        P.dma('sp', lambda e, ms=ms: e.dma_start(out=mixT_d[0:8].rearrange("c p n -> p c n")[:, :, tok0:tok0 + L], in_=ms[:, :, 0:L]),
              reads=[rms_], writes=['mixT_ret_%d' % ci])

    P.op('pool', lambda e: e.memset(gst[:], 0.0), writes=['gs1', 'gs2'])
    for h in range(4):
        P.op('act', lambda e, h=h: e.copy(out=Sbf[:, h, :, :], in_=S32[:, h, :, :]), reads=['S32_%d' % h], writes=['Sbf_%d' % h])
    for c in range(16):
        ret_chunk(64 * c, 64, kdec64, 'kdec64', gL64, True)
    P.dma('sp', lambda e: e.dma_start(out=O["ret_p"].rearrange("h (c p) e -> p h c e", c=2, p=128), in_=S32[:]),
          reads=['S32_%d' % h for h in range(4)], writes=['ret_p'])
    for b in range(4):
        P.dma('sp', lambda e, b=b: e.dma_start(out=S32[:], in_=I["state"][b].rearrange("h (c p) e -> p h c e", c=2, p=128)),
              reads=['ret_p'], writes=['S32_%d' % h for h in range(4)])
        for h in range(4):
            P.op('act', lambda e, h=h: e.copy(out=Sbf[:, h, :, :], in_=S32[:, h, :, :]), reads=['S32_%d' % h], writes=['Sbf_%d' % h])
        ret_chunk(1024 + 32 * b, 32, kdec32, 'kdec32', gL32, True)
        P.dma('sp', lambda e, b=b: e.dma_start(out=O["ret_s"][b].rearrange("h (c p) e -> p h c e", c=2, p=128), in_=S32[:]),
              reads=['S32_%d' % h for h in range(4)], writes=['ret_s_%d' % b])
    if 'mixret' in dbg:
        d = dbg_out('mixret', [8, 128, NTOK], BF16)
        P.dma('sp', lambda e, d=d: e.dma_start(out=d, in_=mixT_d[0:8]), reads=['mixT_ret_%d' % i for i in range(0, 20)])
    P.barrier()
    arena.release(mark12)
    if stop_after <= 2:
        return finish()

    mark25 = arena.mark()
    c64 = SBg("c64", [64, NTOK])
    s64 = SBg("s64", [64, NTOK])
    P.dma('sp', lambda e: e.dma_start(out=c64[:], in_=I["c64T"]), writes=['c64'])
    P.dma('sp', lambda e: e.dma_start(out=s64[:], in_=I["s64T"]), writes=['s64'])
    wq3 = I["w_uq"].rearrange("(c p) n -> p c n", p=128)
    wqs = [SBg("wq0", [128, 4, 256], BF16), SBg("wq1", [128, 4, 256], BF16)]
    qstn = [SBg("qstn0", [128, NTOK], BF16), SBg("qstn1", [128, NTOK], BF16)]
    qstr = [SBg("qstr0", [64, NTOK], BF16), SBg("qstr1", [64, NTOK], BF16)]
    qt1 = SBg("qt1", [64, 512])
    qt2 = SBg("qt2", [64, 512])
    for h in range(8):
        b = h % 2
        WQ, rWQ = wqs[b], 'wq%d' % b
        base = h * 192
        P.dma('pool', lambda e, WQ=WQ, base=base: e.dma_start(out=WQ[:, :, 0:192], in_=wq3[:, :, base:base + 192]), writes=[rWQ + 'a'])
        P.dma('pool', lambda e, WQ=WQ, base=base: e.dma_start(out=WQ[:, :, 192:224], in_=wq3[:, :, base + 160:base + 192]), writes=[rWQ + 'b'])
        P.dma('pool', lambda e, WQ=WQ, base=base: e.dma_start(out=WQ[:, :, 224:256], in_=wq3[:, :, base + 128:base + 160]), writes=[rWQ + 'c'])
        rW = [rWQ + 'a', rWQ + 'b', rWQ + 'c']
        QN, QR, rQN, rQR = qstn[b], qstr[b], 'qstn%d' % b, 'qstr%d' % b
        for (n0, nn) in tokblks:
            cq_res = ['cqT_%d' % t for t in range(NT)]
            for c in range(4):
                P.op('pe', lambda e, c=c, WQ=WQ, n0=n0, nn=nn: e.matmul(psA[:, 0:nn], lhsT=WQ[:, c, 0:128], rhs=cqT_all[:, c, n0:n0 + nn],
                                                                       start=(c == 0), stop=(c == 3)), reads=rW + cq_res, writes=['psA'])
            for c in range(4):
                P.op('pe', lambda e, c=c, WQ=WQ, n0=n0, nn=nn: e.matmul(psB[0:64, 0:nn], lhsT=WQ[:, c, 128:192], rhs=cqT_all[:, c, n0:n0 + nn],
                                                                       start=(c == 0), stop=(c == 3)), reads=rW + cq_res, writes=['psB'])
            for c in range(4):
                P.op('pe', lambda e, c=c, WQ=WQ, n0=n0, nn=nn: e.matmul(psC[0:64, 0:nn], lhsT=WQ[:, c, 192:256], rhs=cqT_all[:, c, n0:n0 + nn],
                                                                       start=(c == 0), stop=(c == 3)), reads=rW + cq_res, writes=['psC'])
            P.op('act', lambda e, QN=QN, n0=n0, nn=nn: e.copy(out=QN[:, n0:n0 + nn], in_=psA[:, 0:nn]), reads=['psA'], writes=[rQN])
            P.op('dve', lambda e, n0=n0, nn=nn: e.tensor_tensor(out=qt1[:, 0:nn], in0=psB[0:64, 0:nn], in1=c64[:, n0:n0 + nn], op=ALU.mult),
                 reads=['psB', 'c64'], writes=['qt1'])
            P.op('dve', lambda e, n0=n0, nn=nn: e.tensor_tensor(out=qt2[:, 0:nn], in0=psC[0:64, 0:nn], in1=s64[:, n0:n0 + nn], op=ALU.mult),
                 reads=['psC', 's64'], writes=['qt2'])
            P.op('pool', lambda e, QR=QR, n0=n0, nn=nn: e.tensor_tensor(out=QR[:, n0:n0 + nn], in0=qt1[:, 0:nn], in1=qt2[:, 0:nn], op=ALU.add),
                 reads=['qt1', 'qt2'], writes=[rQR])
        P.dma('sp', lambda e, QN=QN, h=h: e.dma_start(out=q_d[h, 0:128, :], in_=QN[:]), reads=[rQN], writes=['q_d_n%d' % h])
        P.dma('sp', lambda e, QR=QR, h=h: e.dma_start(out=q_d[h, 128:192, :], in_=QR[:]), reads=[rQR], writes=['q_d_r%d' % h])
    if 'q' in dbg:
        d = dbg_out('q', [8, 192, NTOK], BF16)
        P.dma('sp', lambda e, d=d: e.dma_start(out=d, in_=q_d), reads=['q_d_n%d' % h for h in range(8)] + ['q_d_r%d' % h for h in range(8)])
    P.barrier()
    arena.release(mark25)
    if stop_after <= 2.5:
        return finish()

    mark3 = arena.mark()
    o_mla_all = SBg("o_mla_all", [128, NPT, 1024], BF16)
    ssq = SBg("ssq", [128, NPT, 8])
    P.op('pool', lambda e: e.memset(ssq[:], 0.0), writes=['ssq'])
    mark3a = arena.mark()
    NRK = NREM * 128
    cT_rem = SBg("cT_rem", [128, 4, NRK], BF16)
    kT_rem = SBg("kT_rem", [64, NRK], BF16)
    P.dma('sp', lambda e: e.dma_start(out=cT_rem[:], in_=crem_d), writes=['cT_rem'])
    P.dma('sp', lambda e: e.dma_start(out=kT_rem[:], in_=krem_d), writes=['kT_rem'])
    validcol = SBg("validcol", [128, 64])
    maskb = SBg("maskb", [128, 4, 512], BF16)
    P.dma('sp', lambda e: e.dma_start(out=validcol[:], in_=I["validcol"]), writes=['validcol'])
    P.dma('pool', lambda e: e.dma_start(out=maskb[:], in_=I["maskblk"]), writes=['maskb'])
    K_h = SBg("K_h", [128, NRK], BF16)
    V_h = SBg("V_h", [128, NREM, 130], BF16)
    Ko_h = SBg("Ko_h", [128, 1024], BF16)
    Vo_h = SBg("Vo_h", [128, 8, 130], BF16)
    P.op('pool', lambda e: e.memset(V_h[:], 0.0), writes=['V_h'])
    P.op('pool', lambda e: e.memset(Vo_h[:], 1.0), writes=['Vo_h'])
    P.op('dve', lambda e: e.tensor_copy(out=V_h[:, :, 128:129], in_=validcol[:, 0:NREM].unsqueeze(2)), reads=['validcol', 'V_h'], writes=['V_h'])
    wk3 = I["w_uk"].rearrange("(c p) n -> p c n", p=128)
    wv3 = I["w_uv"].rearrange("(c p) n -> p c n", p=128)
    wuk = [SBg("wuk0", [128, 4, 128], BF16), SBg("wuk1", [128, 4, 128], BF16)]
    wuv = [SBg("wuv0", [128, 4, 128], BF16), SBg("wuv1", [128, 4, 128], BF16)]
    qn = [SBg("qn0", [128, NTOK], BF16), SBg("qn1", [128, NTOK], BF16)]
    qr = [SBg("qr0", [64, NTOK], BF16), SBg("qr1", [64, NTOK], BF16)]
    PTb = [SBg("PTb%d" % i, [128, 512], BF16) for i in range(3)]
    den = SBg("den", [128, 4])
    onorm = SBg("onorm", [128, 128])
    ojunk = SBg("ojunk", [128, 128], BF16)
    sbanks = [(psA, 'psA'), (psB, 'psB'), (psC, 'psC')]
    ckv_res = ['ckvT_%d' % t for t in range(NT)]
    kr_res = ['krT_%d' % t for t in range(NT)]
    si = 0
    for h in range(8):
        b = h % 2
        WK, WV, rWK, rWV = wuk[b], wuv[b], 'wuk%d' % b, 'wuv%d' % b
        QN, QR, rQN, rQR = qn[b], qr[b], 'qn%d' % b, 'qr%d' % b
        P.dma('pool', lambda e, WK=WK, h=h: e.dma_start(out=WK[:], in_=wk3[:, :, h * 128:(h + 1) * 128]), writes=[rWK])
        P.dma('pool', lambda e, WV=WV, h=h: e.dma_start(out=WV[:], in_=wv3[:, :, h * 128:(h + 1) * 128]), writes=[rWV])
        P.dma('sp', lambda e, QN=QN, h=h: e.dma_start(out=QN[:], in_=q_d[h, 0:128, :]), writes=[rQN])
        P.dma('sp', lambda e, QR=QR, h=h: e.dma_start(out=QR[:], in_=q_d[h, 128:192, :]), writes=[rQR])
        for kb in range(NRK // 512 + 2):
            own = kb >= NRK // 512
            src = ckvT_own if own else cT_rem
            k0 = (kb - NRK // 512) * 512 if own else kb * 512
            dstK = Ko_h if own else K_h
            for c in range(4):
                P.op('pe', lambda e, c=c, WK=WK, src=src, k0=k0: e.matmul(psD[:, :], lhsT=WK[:, c, :], rhs=src[:, c, k0:k0 + 512],
                                                                          start=(c == 0), stop=(c == 3)),
                     reads=[rWK] + (ckv_res if own else ['cT_rem']), writes=['psD'])
            P.op('act', lambda e, dstK=dstK, k0=k0: e.copy(out=dstK[:, k0:k0 + 512], in_=psD[:, :]), reads=['psD'], writes=['Ko_h' if own else 'K_h'])
            pv, rpv = (psO[:, 2, :], 'psO2') if kb % 2 == 0 else (psO[:, 3, :], 'psO3')
            for j in range(4):
                for c in range(4):
                    P.op('pe', lambda e, c=c, j=j, WV=WV, src=src, k0=k0, pv=pv: e.matmul(
                        pv[:, j * 128:(j + 1) * 128], lhsT=src[:, c, k0 + j * 128:k0 + (j + 1) * 128], rhs=WV[:, c, :],
                        start=(j == 0 and c == 0), stop=(j == 3 and c == 3)),
                         reads=[rWV] + (ckv_res if own else ['cT_rem']), writes=[rpv])
            g0 = k0 // 128
            if own:
                P.op('dve', lambda e, g0=g0, pv=pv: e.tensor_copy(out=Vo_h[:, g0:g0 + 4, 0:128], in_=pv.rearrange("p (j e) -> p j e", j=4)),
                     reads=[rpv], writes=['Vo_h'])
            else:
                P.op('dve', lambda e, g0=g0, pv=pv: e.tensor_tensor(out=V_h[:, g0:g0 + 4, 0:128], in0=pv.rearrange("p (j e) -> p j e", j=4),
                                                                   in1=validcol[:, g0:g0 + 4].unsqueeze(2).to_broadcast([128, 4, 128]), op=ALU.mult),
                     reads=[rpv, 'validcol'], writes=['V_h'])
        for qb in range(2):
            n0 = qb * 512
            keys = [('r', g) for g in range(NREM)] + [('o', kl) for kl in range(4 * qb + 4)]
            for i, (kind, g) in enumerate(keys):
                (pS, rS) = sbanks[si % 3]
                PT, rPT = PTb[si % 3], 'PTb%d' % (si % 3)
                si += 1
                Ksrc, Tsrc = (K_h, kT_rem) if kind == 'r' else (Ko_h, krT_own)
                kres = ['K_h', 'kT_rem'] if kind == 'r' else (['Ko_h'] + kr_res)
                P.op('pe', lambda e, pS=pS, Ksrc=Ksrc, g=g, QN=QN, n0=n0: e.matmul(pS[:, :], lhsT=Ksrc[:, g * 128:(g + 1) * 128], rhs=QN[:, n0:n0 + 512],
                                                                                 start=True, stop=False), reads=kres + [rQN], writes=[rS])
                P.op('pe', lambda e, pS=pS, Tsrc=Tsrc, g=g, QR=QR, n0=n0: e.matmul(pS[:, :], lhsT=Tsrc[:, g * 128:(g + 1) * 128], rhs=QR[:, n0:n0 + 512],
                                                                                 start=False, stop=True), reads=kres + [rQR], writes=[rS])
                P.op('act', lambda e, pS=pS, PT=PT: e.activation(out=PT[:], in_=pS[:, :], func=AF.Exp, scale=SCALE), reads=[rS], writes=[rPT])
                if kind == 'o' and g >= 4 * qb:
                    P.op('dve', lambda e, PT=PT, g=g, qb=qb: e.tensor_tensor(out=PT[:], in0=PT[:], in1=maskb[:, g - 4 * qb, :], op=ALU.mult),
                         reads=[rPT, 'maskb'], writes=[rPT])
                Vsrc, vres = (V_h, 'V_h') if kind == 'r' else (Vo_h, 'Vo_h')
                for qt in range(4):
                    P.op('pe', lambda e, qt=qt, PT=PT, Vsrc=Vsrc, g=g, i=i, last=(i == len(keys) - 1): e.matmul(
                        psO[:, qt // 2, (qt % 2) * 130:(qt % 2) * 130 + 130], lhsT=PT[:, qt * 128:(qt + 1) * 128], rhs=Vsrc[:, g, :],
                        start=(i == 0 and qt % 2 == 0), stop=(last and qt % 2 == 1)), reads=[rPT, vres], writes=['psO%d' % (qt // 2)])
            for qt in range(4):
                t = qb * 4 + qt
                oa = psO[:, qt // 2, (qt % 2) * 130:(qt % 2) * 130 + 130]
                rO = 'psO%d' % (qt // 2)
                P.op('dve', lambda e, oa=oa, qt=qt: e.reciprocal(out=den[:, qt:qt + 1], in_=oa[:, 128:129]), reads=[rO], writes=['den%d' % qt])
                P.op('dve', lambda e, oa=oa, qt=qt: e.tensor_scalar(out=onorm[:], in0=oa[:, 0:128], scalar1=den[:, qt:qt + 1], scalar2=None, op0=ALU.mult),
                     reads=[rO, 'den%d' % qt], writes=['onorm'])
                P.op('act', lambda e, t=t, h=h: e.activation(out=ojunk[:], in_=onorm[:], func=AF.Square, accum_out=ssq[:, t, h:h + 1]),
                     reads=['onorm', 'ssq'], writes=['ojunk', 'ssq_%d' % t])
                P.op('act', lambda e, t=t, h=h: e.copy(out=o_mla_all[:, t, h * 128:(h + 1) * 128], in_=onorm[:]), reads=['onorm'], writes=['omla_%d' % t])
    if 'omla' in dbg:
        d = dbg_out('omla', [128, NPT, 1024], BF16)
        P.dma('sp', lambda e, d=d: e.dma_start(out=d, in_=o_mla_all[:]), reads=['omla_%d' % t for t in range(NPT)])
    P.barrier()
    arena.release(mark3a)
    if stop_after <= 3:
        return finish()

    mark3c = arena.mark()
    gpc_mla = SBg("gpc_mla", [128, 8])
    P.dma('sp', lambda e: e.dma_start(out=gpc_mla[:], in_=I["gpc_mla"]), writes=['gpc_mla'])
    sst = SBg("sst", [128, 2 * NPT])
    onb = [SBg("onb0", [128, 1024], BF16), SBg("onb1", [128, 1024], BF16)]
    mst3 = [SBg("mst3_0", [128, 8, 128], BF16), SBg("mst3_1", [128, 8, 128], BF16)]
    mixm = mixT_d[8:16].rearrange("c p n -> p c n")
    for t in range(NPT):
        b = t % 2
        ONB, rONB, MS, rMS = onb[b], 'onb%d' % b, mst3[b], 'mst3_%d' % b
        P.op('dve', lambda e, t=t: e.reduce_sum(out=sst[:, t:t + 1], in_=ssq[:, t, :], axis=mybir.AxisListType.X), reads=['ssq'], writes=['sst%d' % t])
        rstd_from_ss(sst[:, t:t + 1], sst[:, NPT + t:NPT + t + 1], 1024, 'sst%d' % t, 'srs%d' % t)
        P.op('dve', lambda e, t=t, ONB=ONB: e.tensor_scalar(out=ONB[:], in0=o_mla_all[:, t, :], scalar1=sst[:, NPT + t:NPT + t + 1], scalar2=None, op0=ALU.mult),
             reads=['srs%d' % t, 'omla'], writes=[rONB])
        transpose_to(lambda c, MS=MS: MS[:, c, :], lambda c, ONB=ONB: ONB[:, c * 128:(c + 1) * 128], 8, rONB, rMS, gain=gpc_mla, pbank=alt('C', 'D'),
                     gain_res='gpc_mla')
        P.dma('sp', lambda e, MS=MS, t=t: e.dma_start(out=mixm[:, :, t * 128:(t + 1) * 128], in_=MS[:]), reads=[rMS], writes=['mixT_mla_%d' % t])

    if stop_after == 3.2:
        return finish()
    wukf = SBg("wukf", [128, 4, 1024], BF16)
    wuvf = SBg("wuvf", [128, 4, 1024], BF16)
    P.dma('pool', lambda e: e.dma_start(out=wukf[:], in_=I["w_uk"].rearrange("(c p) n -> p c n", p=128)), writes=['wukf'])
    P.dma('pool', lambda e: e.dma_start(out=wuvf[:], in_=I["w_uv"].rearrange("(c p) n -> p c n", p=128)), writes=['wuvf'])
    wukT = SBg("wukT", [128, 8, 512], BF16)
    for h in range(8):
        pst, rps = (psC, 'psC') if h % 2 == 0 else (psD, 'psD')
        pv = psbf(pst)
        for cc in range(4):
            P.op('pe', lambda e, h=h, cc=cc, pv=pv: e.transpose(pv[:, cc * 128:(cc + 1) * 128], wukf[:, cc, h * 128:(h + 1) * 128], ident_b[:, :]),
                 reads=['wukf', 'ident_b'], writes=[rps])
        P.op('act', lambda e, h=h, pv=pv: e.copy(out=wukT[:, h, :], in_=pv[:, 0:512]), reads=[rps], writes=['wukT'])
    qsn = SBg("qsn", [128, 8, 128], BF16)
    qsr = SBg("qsr", [64, 8, 128], BF16)
    P.dma('sp', lambda e: e.dma_start(out=qsn[:], in_=q_d[:, 0:128, 1024:1152].rearrange("h d n -> d h n")), writes=['qsn'])
    P.dma('sp', lambda e: e.dma_start(out=qsr[:], in_=q_d[:, 128:192, 1024:1152].rearrange("h d n -> d h n")), writes=['qsr'])
    c_b = SBg("c_b", [128, 32, 512], BF16)
    kr_b = SBg("kr_b", [128, 32, 64], BF16)
    cT_b = SBg("cT_b", [128, 4, PAST], BF16)
    krT_b = SBg("krT_b", [64, PAST], BF16)
    qnb = SBg("qnb", [128, 256], BF16)
    qrb = SBg("qrb", [64, 256], BF16)
    qlat = SBg("qlat", [128, 4, 256], BF16)
    PTs3 = [SBg("PTs%d" % i, [128, 256], BF16) for i in range(3)]
    cnew = SBg("cnew", [32, 512], BF16)
    ones_b = SBg("ones_b", [128, 2], BF16)
    P.op('pool', lambda e: e.memset(ones_b[:], 1.0), writes=['ones_b'])
    dens = SBg("dens", [128, 4])
    ctxn = SBg("ctxn", [128, 2, 512], BF16)
    ctxT = SBg("ctxT", [128, 4, 256], BF16)
    osb = SBg("osb", [32, 1024])
    osn = SBg("osn", [32, 1024], BF16)
    osj = SBg("osj", [32, 1024], BF16)
    oss = SBg("oss", [32, 8])
    P.op('pool', lambda e: e.memset(oss[:], 0.0), writes=['oss'])
    mss = SBg("mss", [128, 8, 32], BF16)
    if stop_after == 3.3:
        return finish()
    for b in range(4):
        t0 = 1024 + 32 * b
        for gq in range(4):
            P.dma('pool', lambda e, b=b, gq=gq: e.dma_start(out=c_b[:, 8 * gq:8 * gq + 8, :],
                                                         in_=I["cache_ckv"][b].rearrange("(g p) c -> p g c", p=128)[:, 8 * gq:8 * gq + 8, :]), writes=['c_b%d' % gq])
            P.dma('pool', lambda e, b=b, gq=gq: e.dma_start(out=kr_b[:, 8 * gq:8 * gq + 8, :],
                                                         in_=I["cache_kr"][b].rearrange("(g p) c -> p g c", p=128)[:, 8 * gq:8 * gq + 8, :]), writes=['kr_b%d' % gq])
        P.dma('sp', lambda e, b=b: e.dma_start(out=cnew[:], in_=own_d[32 * b:32 * b + 32, :]), writes=['cnew'])
        for g in range(32):
            pst, rps = (psC, 'psC') if g % 2 == 0 else (psD, 'psD')
            pv = psbf(pst)
            for cc in range(4):
                P.op('pe', lambda e, g=g, cc=cc, pv=pv: e.transpose(pv[:, cc * 128:(cc + 1) * 128], c_b[:, g, cc * 128:(cc + 1) * 128], ident_b[:, :]),
                     reads=['c_b%d' % (g // 8), 'ident_b'], writes=[rps])
            pk, rpk = (psA, 'psA') if g % 2 == 0 else (psB, 'psB')
            pvk = psbf(pk)
            P.op('pe', lambda e, g=g, pvk=pvk: e.transpose(pvk[0:64, 0:128], kr_b[:, g, :], ident_b[:, :]), reads=['kr_b%d' % (g // 8), 'ident_b'], writes=[rpk])
            P.op('act' if g % 2 else 'dve',
                 (lambda e, g=g, pv=pv: e.copy(out=cT_b[:, :, g * 128:(g + 1) * 128], in_=pv[:, 0:512].rearrange("p (c k) -> p c k", c=4))) if g % 2 else
                 (lambda e, g=g, pv=pv: e.tensor_copy(out=cT_b[:, :, g * 128:(g + 1) * 128], in_=pv[:, 0:512].rearrange("p (c k) -> p c k", c=4))),
                 reads=[rps], writes=['cT_b'])
            P.op('act', lambda e, g=g, pvk=pvk: e.copy(out=krT_b[:, g * 128:(g + 1) * 128], in_=pvk[0:64, 0:128]), reads=[rpk], writes=['krT_b'])
        if stop_after == 3.41:
            return finish()
        P.op('dve', lambda e, b=b: e.tensor_copy(out=qnb[:].rearrange("p (h t) -> p h t", h=8), in_=qsn[:, :, 32 * b:32 * b + 32]), reads=['qsn'], writes=['qnb'])
        P.op('dve', lambda e, b=b: e.tensor_copy(out=qrb[:].rearrange("p (h t) -> p h t", h=8), in_=qsr[:, :, 32 * b:32 * b + 32]), reads=['qsr'], writes=['qrb'])
        for cc in range(4):
            for h in range(8):
                P.op('pe', lambda e, cc=cc, h=h: e.matmul(psA[:, h * 32:(h + 1) * 32], lhsT=wukT[:, h, cc * 128:(cc + 1) * 128], rhs=qnb[:, h * 32:(h + 1) * 32],
                                                         start=True, stop=True), reads=['wukT', 'qnb'], writes=['psA'])
            P.op('act', lambda e, cc=cc: e.copy(out=qlat[:, cc, :], in_=psA[:, 0:256]), reads=['psA'], writes=['qlat'])
        if stop_after == 3.42:
            return finish()
        for g in range(33):
            (pS, rS) = sbanks[g % 3]
            PT, rPT = PTs3[g % 3], 'PTs%d' % (g % 3)
            if g < 32:
                np_ = 128
                for cc in range(4):
                    P.op('pe', lambda e, cc=cc, g=g, pS=pS: e.matmul(pS[:, 0:256], lhsT=cT_b[:, cc, g * 128:(g + 1) * 128], rhs=qlat[:, cc, :],
                                                                    start=(cc == 0), stop=False), reads=['cT_b', 'qlat'], writes=[rS])
                P.op('pe', lambda e, g=g, pS=pS: e.matmul(pS[:, 0:256], lhsT=krT_b[:, g * 128:(g + 1) * 128], rhs=qrb[:, :], start=False, stop=True),
                     reads=['krT_b', 'qrb'], writes=[rS])
            else:
                np_ = 32
                for cc in range(4):
                    P.op('pe', lambda e, cc=cc, pS=pS, t0=t0: e.matmul(pS[0:32, 0:256], lhsT=ckvT_own[:, cc, t0:t0 + 32], rhs=qlat[:, cc, :],
                                                                      start=(cc == 0), stop=False), reads=ckv_res + ['qlat'], writes=[rS])
                P.op('pe', lambda e, pS=pS, t0=t0: e.matmul(pS[0:32, 0:256], lhsT=krT_own[:, t0:t0 + 32], rhs=qrb[:, :], start=False, stop=True),
                     reads=kr_res + ['qrb'], writes=[rS])
            P.op('act', lambda e, pS=pS, PT=PT, np_=np_: e.activation(out=PT[0:np_, :], in_=pS[0:np_, 0:256], func=AF.Exp, scale=SCALE), reads=[rS], writes=[rPT])
            vsrc = (lambda g=g: c_b[:, g, :]) if g < 32 else (lambda: cnew[:, :])
            for j in range(2):
                P.op('pe', lambda e, j=j, g=g, PT=PT, np_=np_, vsrc=vsrc: e.matmul(psO[:, j, :], lhsT=PT[0:np_, j * 128:(j + 1) * 128], rhs=vsrc(),
                                                                                 start=(g == 0), stop=(g == 32)), reads=[rPT, 'c_b%d' % (min(g, 31) // 8), 'cnew'], writes=['psO%d' % j])
                P.op('pe', lambda e, j=j, g=g, PT=PT, np_=np_: e.matmul(psO[:, 2, j:j + 1], lhsT=PT[0:np_, j * 128:(j + 1) * 128], rhs=ones_b[0:np_, 0:1],
                                                                       start=(g == 0 and j == 0), stop=(g == 32 and j == 1)), reads=[rPT, 'ones_b'], writes=['psO2'])
        if stop_after == 3.43:
            return finish()
        for j in range(2):
            P.op('dve', lambda e, j=j: e.reciprocal(out=dens[:, j:j + 1], in_=psO[:, 2, j:j + 1]), reads=['psO2'], writes=['dens%d' % j])
            P.op('dve', lambda e, j=j: e.tensor_scalar(out=ctxn[:, j, :], in0=psO[:, j, :], scalar1=dens[:, j:j + 1], scalar2=None, op0=ALU.mult),
                 reads=['psO%d' % j, 'dens%d' % j], writes=['ctxn'])
        pv = psbf(psD)
        for j in range(2):
            for cc in range(4):
                P.op('pe', lambda e, j=j, cc=cc, pv=pv: e.transpose(pv[:, (j * 4 + cc) * 128:(j * 4 + cc + 1) * 128], ctxn[:, j, cc * 128:(cc + 1) * 128], ident_b[:, :]),
                     reads=['ctxn', 'ident_b'], writes=['psD'])
        for j in range(2):
            P.op('act', lambda e, j=j, pv=pv: e.copy(out=ctxT[:, :, j * 128:(j + 1) * 128], in_=pv[:, j * 512:(j + 1) * 512].rearrange("p (c k) -> p c k", c=4)),
                 reads=['psD'], writes=['ctxT'])
        for hb in range(2):
            pbk, rpb = (psA, 'psA') if hb == 0 else (psB, 'psB')
            for hh in range(4):
                h = hb * 4 + hh
                for cc in range(4):
                    P.op('pe', lambda e, h=h, hh=hh, cc=cc, pbk=pbk: e.matmul(pbk[0:32, hh * 128:(hh + 1) * 128], lhsT=ctxT[:, cc, h * 32:(h + 1) * 32],
                                                                              rhs=wuvf[:, cc, h * 128:(h + 1) * 128], start=(hh == 0 and cc == 0),
                                                                              stop=(hh == 3 and cc == 3)), reads=['ctxT', 'wuvf'], writes=[rpb])
            P.op('act', lambda e, hb=hb, pbk=pbk: e.copy(out=osb[:, hb * 512:(hb + 1) * 512], in_=pbk[0:32, :]), reads=[rpb], writes=['osb'])
        if stop_after == 3.44:
            return finish()
        if 'osm' in dbg:
            P.dma('sp', lambda e, b=b: e.dma_start(out=osm_d[32 * b:32 * b + 32, :], in_=osb[:]), reads=['osb'], writes=['osm_%d' % b])
        P.op('act', lambda e, b=b: e.activation(out=osj[:], in_=osb[:], func=AF.Square, accum_out=oss[:, b:b + 1]), reads=['osb', 'oss'], writes=['osj', 'oss%d' % b])
        rstd_from_ss(oss[:, b:b + 1], oss[:, 4 + b:5 + b], 1024, 'oss%d' % b, 'osr%d' % b)
        P.op('dve', lambda e, b=b: e.tensor_scalar(out=osn[:], in0=osb[:], scalar1=oss[:, 4 + b:5 + b], scalar2=None, op0=ALU.mult),
             reads=['osb', 'osr%d' % b], writes=['osn'])
        transpose_to(lambda c: mss[:, c, :], lambda c: osn[:, c * 128:(c + 1) * 128], 8, 'osn', 'mss', gain=gpc_mla, npart=32, pbank='C', gain_res='gpc_mla')
        P.dma('sp', lambda e, t0=t0: e.dma_start(out=mixm[:, :, t0:t0 + 32], in_=mss[:]), reads=['mss'], writes=['mixT_mla_s%d' % b])
    if 'mixmla' in dbg:
        d = dbg_out('mixmla', [8, 128, NTOK], BF16)
        P.dma('sp', lambda e, d=d: e.dma_start(out=d, in_=mixT_d[8:16]), reads=['mixT_mla_%d' % t for t in range(NPT)] + ['mixT_mla_s%d' % b for b in range(4)])
    if 'osm' in dbg:
        d = dbg_out('osm', [128, 1024])
        P.dma('sp', lambda e, d=d: e.dma_start(out=d, in_=osm_d), reads=['osm_%d' % b for b in range(4)])
    P.barrier()
    arena.release(markG)
    if stop_after <= 3.5:
        return finish()

    h_all = SBg("h_all", [128, NT, D])
    idsT_all = SBg("idsT_all", [128, NT, 128], I32)
    gT_all = SBg("gT_all", [128, NT, 128])
    st4 = SBg("st4", [128, 8 * NT])
    P.op('pool', lambda e: e.memset(st4[:], 0.0), writes=['st4'])
    for t in range(NT):
        P.dma('sp', lambda e, t=t: e.dma_start(out=h_all[:, t, :], in_=I["x_all"][t * 128:(t + 1) * 128, :]), writes=['h_%d' % t])
    mark4 = arena.mark()
    wts = [SBg("wt0b", [128, 16, 256], BF16), SBg("wt1b", [128, 16, 256], BF16)]
    wi = 0
    mark4a = arena.mark()
    mixT_all = SBg("mixT_all", [128, 16, NTOK], BF16)
    P.dma('sp', lambda e: e.dma_start(out=mixT_all[:], in_=mixT_d.rearrange("c p n -> p c n")), writes=['mixT_all'])
    for ct in range(8):
        wt = wts[wi % 2]
        wres = 'wt%d' % (wi % 2)
        wi += 1
        load_w_tile(wt, I["w_out"], ct * 256, 256, wres)
        for t in range(NT):
            pbk = alt('A', 'B')
            for c in range(16):
                P.op('pe', lambda e, c=c, t=t, pbk=pbk, wt=wt: e.matmul(PS[pbk][:, 0:256], lhsT=mixT_all[:, c, t * 128:(t + 1) * 128], rhs=wt[:, c, :],
                                                                       start=(c == 0), stop=(c == 15)), reads=[wres, 'mixT_all'], writes=['ps' + pbk])
            P.op('dve', lambda e, t=t, ct=ct, pbk=pbk: e.tensor_tensor(out=h_all[:, t, ct * 256:(ct + 1) * 256], in0=PS[pbk][:, 0:256],
                                                                      in1=h_all[:, t, ct * 256:(ct + 1) * 256], op=ALU.add),
                 reads=['ps' + pbk, 'h_%d' % t], writes=['h_%d' % t])
    if stop_after == 3.6:
        return finish()
    if 'h1' in dbg:
        d = dbg_out('h1', [128, NT, D])
        P.dma('sp', lambda e, d=d: e.dma_start(out=d, in_=h_all[:]), reads=['h_%d' % t for t in range(NT)])
    P.barrier()
    arena.release(mark4a)

    def norm_T(dstT, gain, gain_res, stat_off, tag, md=None, gbrow=None):
        xsb = [SBg(tag + "xs0", [128, D], BF16), SBg(tag + "xs1", [128, D], BF16)]
        jk = SBg(tag + "jk", [128, D], BF16)
        for t in range(NT):
            b = t % 2
            XS, rXS = xsb[b], tag + 'xs%d' % b
            P.op('act', lambda e, t=t: e.activation(out=jk[:], in_=h_all[:, t, :], func=AF.Square, accum_out=st4[:, stat_off + t:stat_off + t + 1]),
                 reads=['h_%d' % t, 'st4'], writes=[tag + 'jk', tag + 'ss%d' % t])
            rstd_from_ss(st4[:, stat_off + t:stat_off + t + 1], st4[:, stat_off + NT + t:stat_off + NT + t + 1], D, tag + 'ss%d' % t, tag + 'rs%d' % t)
            P.op('dve', lambda e, t=t, XS=XS: e.tensor_scalar(out=XS[:], in0=h_all[:, t, :], scalar1=st4[:, stat_off + NT + t:stat_off + NT + t + 1],
                                                               scalar2=None, op0=ALU.mult), reads=['h_%d' % t, tag + 'rs%d' % t], writes=[rXS])
            transpose_to(lambda c, t=t: dstT[:, c, t * 128:(t + 1) * 128], lambda c, XS=XS: XS[:, c * 128:(c + 1) * 128], 16, rXS, tag + 'T_%d' % t,
                         gain=gain, pbank=alt('C', 'D'), gain_res=gain_res)
            if md is not None:
                P.op('pool', lambda e, XS=XS: e.tensor_tensor(out=jk[:], in0=XS[:], in1=gbrow[:], op=ALU.mult), reads=[rXS, 'gbrow', tag + 'jk'], writes=[tag + 'jk'])
                P.dma('sp', lambda e, t=t: e.dma_start(out=md[t * 128:(t + 1) * 128, :], in_=jk[:]), reads=[tag + 'jk'], writes=['m_d_%d' % t])

    qpT_all = SBg("qpT_all", [128, 16, NTOK], BF16)
    mark4b = arena.mark()
    mT_all = SBg("mT_all", [128, 16, NTOK], BF16)
    gpc_ffn = SBg("gpc_ffn", [128, 16])
    gbrow = SBg("gbrow", [128, D], BF16)
    P.dma('sp', lambda e: e.dma_start(out=gpc_ffn[:], in_=I["gpc_ffn"]), writes=['gpc_ffn'])
    P.dma('pool', lambda e: e.dma_start(out=gbrow[:], in_=I["gb_ffn"]), writes=['gbrow'])
    norm_T(mT_all, gpc_ffn, 'gpc_ffn', 0, 'n2', md=m_d, gbrow=gbrow)
    mT_res = ['n2T_%d' % t for t in range(NT)]
    if stop_after == 3.7:
        return finish()
    for ct in range(8):
        wt = wts[wi % 2]
        wres = 'wt%d' % (wi % 2)
        wi += 1
        load_w_tile(wt, I["w_pq"], ct * 256, 256, wres)
        for blk in range(2):
            for (n0, nn) in tokblks:
                pbk = alt('A', 'B')
                for c in range(16):
                    P.op('pe', lambda e, c=c, blk=blk, pbk=pbk, wt=wt, n0=n0, nn=nn: e.matmul(
                        PS[pbk][:, 0:nn], lhsT=wt[:, c, blk * 128:(blk + 1) * 128], rhs=mT_all[:, c, n0:n0 + nn], start=(c == 0), stop=(c == 15)),
                        reads=[wres] + mT_res, writes=['ps' + pbk])
                P.op('act', lambda e, ct=ct, blk=blk, pbk=pbk, n0=n0, nn=nn: e.copy(out=qpT_all[:, ct * 2 + blk, n0:n0 + nn], in_=PS[pbk][:, 0:nn]),
                     reads=['ps' + pbk], writes=['qpT_%d' % (ct * 2 + blk)])
    qp_res = ['qpT_%d' % i for i in range(16)]
    P.barrier()
    arena.release(mark4b)
    if stop_after == 3.8:
        return finish()
    skn = SBg("skn", [128, 16, 128], BF16)
    skT = SBg("skT", [128, 16, 128], BF16)
    P.dma('pool', lambda e: e.dma_start(out=skn[:], in_=I["subk"].rearrange("b k d -> k b d")), writes=['skn'])
    for half in range(2):
        pv = psbf(psC if half == 0 else psD)
        rps = 'psC' if half == 0 else 'psD'
        for j in range(8):
            P.op('pe', lambda e, j=j, half=half, pv=pv: e.transpose(pv[:, j * 128:(j + 1) * 128], skn[:, half * 8 + j, :], ident_b[:, :]),
                 reads=['skn', 'ident_b'], writes=[rps])
        P.op('act', lambda e, half=half, pv=pv: e.copy(out=skT[:, half * 8:(half + 1) * 8, :], in_=pv[:, :].rearrange("p (j k) -> p j k", j=8)),
             reads=[rps], writes=['skT'])
    if stop_after == 3.9:
        return finish()
    sc = SBg("sc", [128, 16, 128])
    V16 = SBg("V16", [128, 16, 16])
    Iu = SBg("Iu", [128, 16, 16], U32)
    If_ = SBg("If", [128, 16, 16])
    wk1 = SBg("wk1", [128, 128])
    cand = SBg("cand", [128, 8, 16, 16])
    cid = SBg("cid", [128, 8, 16, 16])
    wk2 = SBg("wk2", [128, 256])
    tv = SBg("tv", [128, 8, 16])
    ids = SBg("ids", [128, 128])
    ids2 = SBg("ids2", [128, 128])
    tb16 = SBg("tb16", [128, 3, 128], BF16)
    lo32 = SBg("lo32", [128, 128])
    jk3 = SBg("jk3", [128, 256])
    negmx = SBg("negmx", [128, 8])
    ew = SBg("ew", [128, 8, 16])
    Zs = SBg("Zs", [128, 16])
    gw = SBg("gw", [128, 8, 16])
    for t in ([1] if 'only1' in dbg else [0, 0] if 'twice0' in dbg else range(NT)):
        for blk in range(16):
            P.op('pe', lambda e, blk=blk, t=t: e.matmul(psO[:, blk // 4, (blk % 4) * 128:(blk % 4 + 1) * 128], lhsT=qpT_all[:, blk, t * 128:(t + 1) * 128],
                                                       rhs=skT[:, blk, :], start=(blk % 4 == 0), stop=(blk % 4 == 3)),
                 reads=qp_res + ['skT'], writes=['psO%d' % (blk // 4)])
        for q in range(4):
            P.op('act', lambda e, q=q: e.copy(out=sc[:, 4 * q:4 * q + 4, :], in_=psO[:, q, :].rearrange("p (b k) -> p b k", b=4)),
                 reads=['psO%d' % q], writes=['sc%d' % q])
        if stop_after == 3.91:
            return finish()
        for blk in range(16):
            rsc = 'sc%d' % (blk // 4)
            P.op('dve', lambda e, blk=blk: e.max(out=V16[:, blk, 0:8], in_=sc[:, blk, :]), reads=[rsc], writes=['V16'])
            P.op('dve', lambda e, blk=blk: e.max_index(out=Iu[:, blk, 0:8], in_max=V16[:, blk, 0:8], in_values=sc[:, blk, :]), reads=[rsc, 'V16'], writes=['Iu'])
            P.op('dve', lambda e, blk=blk: e.match_replace(out=wk1[:], in_to_replace=V16[:, blk, 0:8], in_values=sc[:, blk, :], imm_value=-1e30),
                 reads=[rsc, 'V16'], writes=['wk1'])
            P.op('dve', lambda e, blk=blk: e.max(out=V16[:, blk, 8:16], in_=wk1[:]), reads=['wk1'], writes=['V16'])
            P.op('dve', lambda e, blk=blk: e.max_index(out=Iu[:, blk, 8:16], in_max=V16[:, blk, 8:16], in_values=wk1[:]), reads=['wk1', 'V16'], writes=['Iu'])
        if stop_after == 3.92:
            return finish()
        P.op('dve', lambda e: e.tensor_copy(out=If_[:], in_=Iu[:]), reads=['Iu'], writes=['If'])
        V4 = V16[:].rearrange("p (h s) k -> p h s k", s=2)
        I4 = If_[:].rearrange("p (h s) k -> p h s k", s=2)
        P.op('dve', lambda e, V4=V4: e.tensor_tensor(out=cand[:], in0=V4[:, :, 0, :].unsqueeze(3).to_broadcast([128, 8, 16, 16]),
                                                      in1=V4[:, :, 1, :].unsqueeze(2).to_broadcast([128, 8, 16, 16]), op=ALU.add), reads=['V16'], writes=['cand'])
        if stop_after == 3.93:
            return finish()
        P.op('pool', lambda e: e.memset(ids[:], 0.0), writes=['ids'])
        P.op('pool', lambda e: e.memset(ids2[:], 0.0), writes=['ids2'])
        for h in range(8):
            ch = cand[:, h, :, :].rearrange("p j k -> p (j k)")
            ih = cid[:, h, :, :].rearrange("p j k -> p (j k)")
            P.op('dve', lambda e, h=h, ch=ch: e.max(out=tv[:, h, 0:8], in_=ch), reads=['cand'], writes=['tv'])
            P.op('dve', lambda e, h=h, ch=ch: e.match_replace(out=wk2[:], in_to_replace=tv[:, h, 0:8], in_values=ch, imm_value=-1e30), reads=['cand', 'tv'], writes=['wk2'])
            P.op('dve', lambda e, h=h: e.max(out=tv[:, h, 8:16], in_=wk2[:]), reads=['wk2'], writes=['tv'])
            c3 = cand[:, h, :, :]
            i1b = I4[:, h, 0, :].unsqueeze(2).to_broadcast([128, 16, 16])
            i2b = I4[:, h, 1, :].unsqueeze(1).to_broadcast([128, 16, 16])
            j3 = jk3[:].rearrange("p (j k) -> p j k", j=16)
            for r_ in range(16):
                P.op('dve', lambda e, h=h, r_=r_, c3=c3, i1b=i1b, j3=j3: e.scalar_tensor_tensor(out=j3, in0=c3, scalar=tv[:, h, r_:r_ + 1], in1=i1b, op0=ALU.is_equal,
                                                                                          op1=ALU.mult, accum_out=ids[:, h * 16 + r_:h * 16 + r_ + 1]),
                     reads=['cand', 'If', 'tv', 'ids'], writes=['jk3', 'ids'])
                P.op('dve', lambda e, h=h, r_=r_, c3=c3, i2b=i2b, j3=j3: e.scalar_tensor_tensor(out=j3, in0=c3, scalar=tv[:, h, r_:r_ + 1], in1=i2b, op0=ALU.is_equal,
                                                                                          op1=ALU.mult, accum_out=ids2[:, h * 16 + r_:h * 16 + r_ + 1]),
                     reads=['cand', 'If', 'tv', 'ids2'], writes=['jk3', 'ids2'])
        if stop_after == 3.94:
            return finish()
        P.op('dve', lambda e: e.tensor_tensor(out=gw[:], in0=tv[:], in1=tv[:, :, 0:1].to_broadcast([128, 8, 16]), op=ALU.subtract), reads=['tv', 'gw'], writes=['gw'])
        P.op('act', lambda e: e.activation(out=ew[:], in_=gw[:], func=AF.Exp), reads=['gw', 'ew'], writes=['ew'])
        P.op('dve', lambda e: e.reduce_sum(out=Zs[:, 0:8], in_=ew[:], axis=mybir.AxisListType.X), reads=['ew'], writes=['Zs'])
        P.op('dve', lambda e: e.reciprocal(out=Zs[:, 8:16], in_=Zs[:, 0:8]), reads=['Zs'], writes=['rZ'])
        P.op('dve', lambda e: e.tensor_tensor(out=gw[:], in0=ew[:], in1=Zs[:, 8:16].unsqueeze(2).to_broadcast([128, 8, 16]), op=ALU.mult),
             reads=['ew', 'rZ'], writes=['gw'])
        P.op('dve', lambda e: e.tensor_scalar_min(out=ids[:], in0=ids[:], scalar1=127.0), reads=['ids'], writes=['ids'])
        P.op('dve', lambda e: e.tensor_scalar_min(out=ids2[:], in0=ids2[:], scalar1=127.0), reads=['ids2'], writes=['ids2'])
        P.op('act', lambda e: e.copy(out=tb16[:, 0, :], in_=ids[:]), reads=['ids'], writes=['tb16'])
        P.op('act', lambda e: e.copy(out=tb16[:, 1, :], in_=ids2[:]), reads=['ids2'], writes=['tb16'])
        P.op('act', lambda e: e.copy(out=tb16[:, 2, :], in_=gw[:].rearrange("p h k -> p (h k)")), reads=['gw'], writes=['tb16'])
        pvA = psbf(psA)
        for j in range(3):
            P.op('pe', lambda e, j=j, pvA=pvA: e.transpose(pvA[:, j * 128:(j + 1) * 128], tb16[:, j, :], ident_b[:, :]), reads=['tb16', 'ident_b'], writes=['psA'])
        P.op('act', lambda e, pvA=pvA: e.copy(out=lo32[:], in_=pvA[:, 128:256]), reads=['psA'], writes=['lo32'])
        P.op('dve', lambda e, t=t, pvA=pvA: e.scalar_tensor_tensor(out=idsT_all[:, t, :], in0=pvA[:, 0:128], scalar=128.0, in1=lo32[:], op0=ALU.mult, op1=ALU.add),
             reads=['psA', 'lo32'], writes=['idsT_%d' % t])
        P.op('act', lambda e, t=t, pvA=pvA: e.copy(out=gT_all[:, t, :], in_=pvA[:, 256:384]), reads=['psA'], writes=['gT_%d' % t])
        if stop_after == 3.96:
            return finish()
        if 'tilebar' in dbg:
            P.barrier()
        if stop_after == 3.97 and t == 1:
            return finish()
        if stop_after == 3.98 and t == 4:
            return finish()
    if 'ids' in dbg:
        d = dbg_out('idsT', [128, NT, 128], I32)
        P.dma('sp', lambda e, d=d: e.dma_start(out=d, in_=idsT_all[:]), reads=['idsT_%d' % t for t in range(NT)])
        d = dbg_out('gT', [128, NT, 128])
        P.dma('sp', lambda e, d=d: e.dma_start(out=d, in_=gT_all[:]), reads=['gT_%d' % t for t in range(NT)])
    P.barrier()
    arena.release(mark4)
    if stop_after <= 4:
        return finish()

    mark5 = arena.mark()
    iota_f = SBg("iota_f", [128, 128])
    P.dma('sp', lambda e: e.dma_start(out=iota_f[:], in_=I["iota"]), writes=['iota_f'])
    NG = 5
    gbuf = [SBg("gbuf%d" % i, [128, D]) for i in range(NG)]
    mbuf = [SBg("mbuf%d" % i, [128, D], BF16) for i in range(NG)]
    jk5 = SBg("jk5", [128, D], BF16)
    hdT = SBg("hdT", [128, 128])
    aT5 = SBg("aT5", [128, 128])
    asel = [SBg("asel%d" % i, [128, 128], BF16) for i in range(NG)]
    v16 = [SBg("v16_%d" % i, [128, D], BF16) for i in range(NG)]
    gi = 0
    for t in range(NT if stop_after > 4.5 else 1):
        P.op('pool', lambda e: e.memset(hdT[:], 0.0), writes=['hdT'])
        for n in range(128):
            G, rG = gbuf[gi % NG], 'gbuf%d' % (gi % NG)
            M, rM = mbuf[gi % NG], 'mbuf%d' % (gi % NG)
            gi += 1
            P.dma('pool', lambda e, G=G, t=t, n=n: e.indirect_dma_start(out=G[:, :], out_offset=None, in_=I["u_tab"][:, :],
                                                                       in_offset=bass.IndirectOffsetOnAxis(ap=idsT_all[:, t, n:n + 1], axis=0)),
                  reads=['idsT_%d' % t], writes=[rG])
            row = t * 128 + n
            P.dma('sp', lambda e, M=M, row=row: e.dma_start(out=M[:], in_=m_d[row:row + 1, :].to_broadcast([128, D])), writes=[rM])
            P.op('dve', lambda e, G=G, M=M, n=n: e.scalar_tensor_tensor(out=jk5[:], in0=G[:], scalar=1.0, in1=M[:], op0=ALU.mult, op1=ALU.mult,
                                                                         accum_out=hdT[:, n:n + 1]), reads=[rG, rM, 'hdT'], writes=['jk5', 'hdT'])
        P.op('act', lambda e: e.activation(out=aT5[:], in_=hdT[:], func=AF.Gelu), reads=['hdT'], writes=['aT5'])
        P.op('dve', lambda e, t=t: e.tensor_tensor(out=aT5[:], in0=aT5[:], in1=gT_all[:, t, :], op=ALU.mult), reads=['aT5', 'gT_%d' % t], writes=['aT5'])
        for n in range(128):
            G, rG = gbuf[gi % NG], 'gbuf%d' % (gi % NG)
            A, rA = asel[gi % NG], 'asel%d' % (gi % NG)
            gi += 1
            P.dma('pool', lambda e, G=G, t=t, n=n: e.indirect_dma_start(out=G[:, :], out_offset=None, in_=I["v_tab"][:, :],
                                                                       in_offset=bass.IndirectOffsetOnAxis(ap=idsT_all[:, t, n:n + 1], axis=0)),
                  reads=['idsT_%d' % t], writes=[rG])
            P.op('dve', lambda e, A=A, n=n: e.scalar_tensor_tensor(out=A[:], in0=iota_f[:], scalar=float(n), in1=aT5[:], op0=ALU.is_equal, op1=ALU.mult),
                 reads=['iota_f', 'aT5'], writes=[rA])
            V16b, rV = v16[gi % NG], 'v16_%d' % (gi % NG)
            P.op('act', lambda e, G=G, V16b=V16b: e.copy(out=V16b[:], in_=G[:]), reads=[rG], writes=[rV])
            for q in range(4):
                P.op('pe', lambda e, A=A, V16b=V16b, q=q, n=n: e.matmul(psO[:, q, :], lhsT=A[:], rhs=V16b[:, q * 512:(q + 1) * 512], start=(n == 0), stop=(n == 127)),
                     reads=[rA, rV], writes=['psO%d' % q])
        if 'peer' in dbg:
            if t == 0:
                dpe = dbg_out('peer', [128, NT, D])
                pst = SBg("pst", [128, D])
            for q in range(4):
                P.op('act', lambda e, q=q: e.copy(out=pst[:, q * 512:(q + 1) * 512], in_=psO[:, q, :]), reads=['psO%d' % q], writes=['pst'])
            P.dma('sp', lambda e, t=t: e.dma_start(out=dpe[:, t, :], in_=pst[:]), reads=['pst'], writes=['dpe%d' % t])
        for q in range(4):
            P.op('dve', lambda e, t=t, q=q: e.tensor_tensor(out=h_all[:, t, q * 512:(q + 1) * 512], in0=psO[:, q, :], in1=h_all[:, t, q * 512:(q + 1) * 512], op=ALU.add),
                 reads=['psO%d' % q, 'h_%d' % t], writes=['h_%d' % t])
    P.barrier()
    arena.release(mark5)
    if stop_after <= 5:
        return finish()

    mark6 = arena.mark()
    wts = [SBg("wt0c", [128, 16, 256], BF16), SBg("wt1c", [128, 16, 256], BF16)]
    wpp = [SBg("wpp0", [128, 2, 256], BF16), SBg("wpp1", [128, 2, 256], BF16)]
    n3T_all = SBg("n3T_all", [128, 16, NTOK], BF16)
    pT_all = SBg("pT_all", [128, 2, NTOK], BF16)
    gpc_ple = SBg("gpc_ple", [128, 16])
    P.dma('sp', lambda e: e.dma_start(out=gpc_ple[:], in_=I["gpc_ple"]), writes=['gpc_ple'])
    mark6a = arena.mark()
    norm_T(n3T_all, gpc_ple, 'gpc_ple', 2 * NT, 'n3')
    n3_res = ['n3T_%d' % t for t in range(NT)]
    pbs = [SBg("pbs0", [128, 256], BF16), SBg("pbs1", [128, 256], BF16)]
    for t in range(NT):
        PB, rPB = pbs[t % 2], 'pbs%d' % (t % 2)
        P.dma('pool', lambda e, PB=PB, t=t: e.dma_start(out=PB[:], in_=I["p_all"][t * 128:(t + 1) * 128, :]), writes=[rPB])
        transpose_to(lambda c, t=t: pT_all[:, c, t * 128:(t + 1) * 128], lambda c, PB=PB: PB[:, c * 128:(c + 1) * 128], 2, rPB, 'pT_%d' % t,
                     gain=None, pbank=alt('C', 'D'))
    P.barrier()
    arena.release(mark6a)
    gts = [SBg("gts0", [128, 256]), SBg("gts1", [128, 256])]
    wp3 = I["w_pp"].rearrange("(c p) n -> p c n", p=128)
    wi = 0
    for ct in range(8):
        wt, wres = wts[wi % 2], 'wt%d' % (wi % 2)
        WP, rWP = wpp[wi % 2], 'wpp%d' % (wi % 2)
        wi += 1
        load_w_tile(wt, I["w_pg"], ct * 256, 256, wres)
        P.dma('pool', lambda e, WP=WP, ct=ct: e.dma_start(out=WP[:], in_=wp3[:, :, ct * 256:(ct + 1) * 256]), writes=[rWP])
        for t in range(NT):
            for c in range(16):
                P.op('pe', lambda e, c=c, t=t, wt=wt: e.matmul(psA[:, 0:256], lhsT=n3T_all[:, c, t * 128:(t + 1) * 128], rhs=wt[:, c, :],
                                                              start=(c == 0), stop=(c == 15)), reads=[wres, 'n3T_%d' % t], writes=['psA'])
            for c in range(2):
                P.op('pe', lambda e, c=c, t=t, WP=WP: e.matmul(psB[:, 0:256], lhsT=pT_all[:, c, t * 128:(t + 1) * 128], rhs=WP[:, c, :],
                                                              start=(c == 0), stop=(c == 1)), reads=[rWP, 'pT_%d' % t], writes=['psB'])
            GT, rGT = gts[t % 2], 'gts%d' % (t % 2)
            P.op('act', lambda e, GT=GT: e.activation(out=GT[:], in_=psA[:, 0:256], func=AF.Sigmoid), reads=['psA'], writes=[rGT])
            P.op('dve', lambda e, GT=GT: e.tensor_tensor(out=GT[:], in0=psB[:, 0:256], in1=GT[:], op=ALU.mult), reads=['psB', rGT], writes=[rGT])
            P.op('pool', lambda e, GT=GT, t=t, ct=ct: e.tensor_tensor(out=h_all[:, t, ct * 256:(ct + 1) * 256], in0=h_all[:, t, ct * 256:(ct + 1) * 256], in1=GT[:], op=ALU.add),
                 reads=[rGT, 'h_%d' % t], writes=['h_%d' % t])
    P.barrier()
    arena.release(mark6)
    gb_fin = SBg("gb_fin", [128, D])
    P.dma('sp', lambda e: e.dma_start(out=gb_fin[:], in_=I["gb_fin"]), writes=['gb_fin'])
    yts = [SBg("yt0", [128, D]), SBg("yt1", [128, D])]
    jk6 = SBg("jk6", [128, D], BF16)
    for t in range(NT):
        so = 4 * NT
        P.op('act', lambda e, t=t: e.activation(out=jk6[:], in_=h_all[:, t, :], func=AF.Square, accum_out=st4[:, so + t:so + t + 1]),
             reads=['h_%d' % t, 'st4'], writes=['jk6', 'fss%d' % t])
        rstd_from_ss(st4[:, so + t:so + t + 1], st4[:, so + NT + t:so + NT + t + 1], D, 'fss%d' % t, 'frs%d' % t)
        Y, rY = yts[t % 2], 'yt%d' % (t % 2)
        P.op('dve', lambda e, t=t, Y=Y: e.scalar_tensor_tensor(out=Y[:], in0=h_all[:, t, :], scalar=st4[:, so + NT + t:so + NT + t + 1], in1=gb_fin[:],
                                                                op0=ALU.mult, op1=ALU.mult), reads=['h_%d' % t, 'frs%d' % t, 'gb_fin'], writes=[rY])
        P.dma('sp', lambda e, t=t, Y=Y: e.dma_start(out=O["y_all"][t * 128:(t + 1) * 128, :], in_=Y[:]), reads=[rY], writes=['y_%d' % t])
    return finish()


def _pc(g, n):
    return np.ascontiguousarray(np.asarray(g, np.float32).reshape(n, 128).T)


def _bc(g):
    g = np.asarray(g, np.float32).reshape(1, -1)
    return np.ascontiguousarray(np.broadcast_to(g, (128, g.shape[1])))


def make_consts(core):
    f32 = np.float32
    lin = np.linspace(np.log(f32(1.0 / 32)), np.log(f32(1.0 / 512)), 4, dtype=f32)
    log_g = np.log1p(-np.exp(lin)).astype(f32)
    pos = np.concatenate([1024 * core + np.arange(1024), np.tile(PAST + np.arange(32), 4)]).astype(f32)
    inv128 = (f32(10000.0) ** (-np.arange(0, 256, 2, dtype=f32) / f32(256))).astype(f32)
    inv32 = (f32(10000.0) ** (-np.arange(0, 64, 2, dtype=f32) / f32(64))).astype(f32)
    a128 = (pos[:, None] * inv128[None, :]).astype(f32)
    a32 = (pos[:, None] * inv32[None, :]).astype(f32)
    c = {}
    c["cosT128"] = np.ascontiguousarray(np.cos(a128).T.astype(f32))
    c["sinT128"] = np.ascontiguousarray(np.sin(a128).T.astype(f32))
    c32, s32 = np.cos(a32).astype(f32), np.sin(a32).astype(f32)
    c["c64T"] = np.ascontiguousarray(np.concatenate([c32.T, c32.T], 0))
    c["s64T"] = np.ascontiguousarray(np.concatenate([-s32.T, s32.T], 0))
    c["cos_tok"] = np.ascontiguousarray(c32.reshape(NT, 128, 32).transpose(1, 0, 2))
    c["sin_tok"] = np.ascontiguousarray(s32.reshape(NT, 128, 32).transpose(1, 0, 2))
    l = np.arange(64)
    diff = (l[None, :] - l[:, None]).astype(f32)
    dm = np.zeros((64, 4, 64), f32)
    for h in range(4):
        dm[:, h, :] = np.where(diff >= 0, np.exp(np.maximum(diff, 0) * log_g[h]), 0.0) * f32(256 ** -0.5)
    c["dmaskT"] = dm
    c["inner"] = np.exp((l + 1).astype(f32)[:, None] * log_g[None, :]).astype(f32)
    c["kdec64"] = (np.exp((63 - l).astype(f32)[:, None] * log_g[None, :]) * f32(256 ** -0.5)).astype(f32)
    k32 = np.zeros((64, 4), f32)
    k32[:32] = np.exp((31 - l[:32]).astype(f32)[:, None] * log_g[None, :]) * f32(256 ** -0.5)
    c["kdec32"] = k32
    posr = np.arange(NREM * 128).astype(f32)
    ar128 = (posr[:, None] * inv128[None, :]).astype(f32)
    ar32 = (posr[:, None] * inv32[None, :]).astype(f32)
    rt = np.concatenate([np.cos(ar128), np.sin(ar128), np.cos(ar32), np.sin(ar32)], 1).astype(f32)
    c["rtab"] = np.ascontiguousarray(rt.reshape(NREM, 128, 320))
    T0 = 1024 * core
    tt = np.arange(NREM * 128)
    wt = np.where(tt[:, None] < T0, np.exp(np.maximum(T0 - 1 - tt, 0).astype(f32)[:, None] * log_g[None, :]), 0.0).astype(f32) * f32(256 ** -0.5)
    c["wtab"] = np.ascontiguousarray(wt.reshape(NREM, 128, 4).transpose(1, 0, 2))
    vc = (np.arange(64) < 8 * core).astype(f32)
    c["validcol"] = np.ascontiguousarray(np.broadcast_to(vc[None], (128, 64)))
    k = np.arange(128)[:, None]
    q = np.arange(512)[None, :]
    mb = np.zeros((128, 4, 512), f32)
    for j in range(4):
        qt = q // 128
        mb[:, j, :] = np.where(qt > j, 1.0, np.where(qt < j, 0.0, ((k // 64) <= ((q % 128) // 64)).astype(f32)))
    c["maskblk"] = mb
    c["ident"] = np.eye(128, dtype=f32)
    c["iota"] = np.ascontiguousarray(np.broadcast_to(np.arange(128, dtype=f32)[None], (128, 128)))
    c["gL64"] = np.exp(f32(64) * log_g).astype(f32)
    c["gL32"] = np.exp(f32(32) * log_g).astype(f32)
    return c


def make_in_maps(inp, small=()):
    f = lambda a: np.ascontiguousarray(np.asarray(a, np.float32))
    shared = {
        "w_in": f(inp["w_in"][0]), "w_uq": f(inp["w_uq"][0]).reshape(512, 1536),
        "w_uk": f(inp["w_uk"][0]).reshape(512, 1024), "w_uv": f(inp["w_uv"][0]).reshape(512, 1024),
        "w_out": f(inp["w_out"][0]), "w_pq": f(inp["w_pq"][0]), "subk": f(inp["sub_keys"][0]).reshape(16, 128, 128),
        "u_tab": f(inp["u_tab"][0]), "v_tab": f(inp["v_tab"][0]), "w_pg": f(inp["w_ple_gate"][0]), "w_pp": f(inp["w_ple_proj"][0]),
        "gpc_mix": _pc(inp["g_mix"][0], 16), "gpc_ffn": _pc(inp["g_ffn"][0], 16), "gpc_ple": _pc(inp["g_ple"][0], 16),
        "gpc_q": _pc(inp["g_q"][0], 4), "gpc_mla": _pc(inp["g_mla_out"][0], 8),
        "gb_kv": _bc(inp["g_kv"][0]), "gb_ret": _bc(inp["g_ret_out"][0]), "gb_fin": _bc(inp["g_final"]), "gb_ffn": _bc(inp["g_ffn"][0]),
    }
    maps = []
    xp, xsm = f(inp["x_prompt"][0]), f(inp["x_sample"])
    pp, psm = f(inp["p_prompt"][0, 0]), f(inp["p_sample"][0])
    for c in range(NCORES):
        m = dict(shared)
        m["x_all"] = np.concatenate([xp[1024 * c:1024 * (c + 1)], xsm[4 * c:4 * c + 4].reshape(128, D)], 0)
        m["p_all"] = np.concatenate([pp[1024 * c:1024 * (c + 1)], psm[4 * c:4 * c + 4].reshape(128, 256)], 0)
        m["x_full"] = xp[:NREM * 128]
        m["cache_ckv"] = f(inp["cache_ckv"][0, 4 * c:4 * c + 4])
        m["cache_kr"] = f(inp["cache_krope"][0, 4 * c:4 * c + 4])
        m["state"] = f(inp["state_ret"][0, 4 * c:4 * c + 4])
        cc = make_consts(c)
        for name, shape, dt in INPUT_SPECS:
            if name in cc:
                m[name] = cc[name]
        maps.append({name: (m[name][:128] if name in small else m[name]) for name, _, _ in INPUT_SPECS})
    return maps


_CACHE = {}


def kernel(**inputs):
    if "nc" not in _CACHE:
        _CACHE["nc"] = build_program()[0]
    nc = _CACHE["nc"]
    maps = make_in_maps(inputs)
    res = run_bass_kernel_spmd(nc, maps, core_ids=list(range(NCORES)))
    R = res.results
    y = np.stack([R[c]["y_all"] for c in range(NCORES)])
    ckv = np.stack([R[c]["ckv_all"] for c in range(NCORES)])
    kr = np.stack([R[c]["kr_all"] for c in range(NCORES)])
    y_prompt = y[:, :1024].reshape(1, SEQ, D)
    y_sample = y[:, 1024:].reshape(32, 32, D)
    ckv_prompt = ckv[:, :1024].reshape(1, 1, SEQ, 512)
    kr_prompt = kr[:, :1024].reshape(1, 1, SEQ, 64)
    ret_prompt = R[NCORES - 1]["ret_p"].reshape(1, 1, 4, 256, 256)
    ckv_sample = ckv[:, 1024:].reshape(1, 32, 32, 512)
    kr_sample = kr[:, 1024:].reshape(1, 32, 32, 64)
    ret_sample = np.stack([R[c]["ret_s"] for c in range(NCORES)]).reshape(1, 32, 4, 256, 256)
    return tuple(np.ascontiguousarray(a.astype(np.float32)) for a in
                 (y_prompt, y_sample, ckv_prompt, kr_prompt, ret_prompt, ckv_sample, kr_sample, ret_sample))
```
